# Optimizing a Trainium2 kernel written in Bass

```python
import math
import jax, jax.numpy as jnp
from jax import lax
import numpy as np

D_MODEL = 1024
BATCH = 8
SEQ = 8192
DEPTH = 2

HEAD_DIM = 64
A_HEADS = 4
IDX_HEADS = 8
IDX_DIM = 64
DSA_TOPK_MAX = 256
DSA_IDX_SCALE = (IDX_HEADS ** -0.5) * (IDX_DIM ** -0.5)
B_HEADS = 4
MOBA_BLOCK = 256
MOBA_TOPK = 3
MOBA_Q_CHUNK = 64
C_HEADS = 8
Q_BLOCK = 128
D_FF = 2816
ROPE_THETA = 10000.0
RMS_EPS = 1e-6

A_WIDTH = A_HEADS * HEAD_DIM
B_WIDTH = B_HEADS * HEAD_DIM
C_WIDTH = C_HEADS * HEAD_DIM
MIX_WIDTH = A_WIDTH + B_WIDTH + C_WIDTH
IN_SIZES = (A_WIDTH, HEAD_DIM, HEAD_DIM, IDX_HEADS * IDX_DIM, IDX_DIM, IDX_HEADS,
            B_WIDTH, B_WIDTH, B_WIDTH,
            C_WIDTH, C_WIDTH, C_WIDTH, C_HEADS, C_WIDTH)
IN_WIDTH = sum(IN_SIZES)

kernel_name = "hybrid_dsa_moba_fox_macaron"


def rms_norm(x, g):
    xf = x.astype(jnp.float32)
    y = xf * lax.rsqrt(jnp.mean(xf * xf, axis=-1, keepdims=True) + RMS_EPS)
    return (y * g.astype(jnp.float32)).astype(x.dtype)


def rope_tables(seq, dim):
    inv = 1.0 / (ROPE_THETA ** (jnp.arange(0, dim, 2, dtype=jnp.float32) / dim))
    ang = jnp.arange(seq, dtype=jnp.float32)[:, None] * inv[None, :]
    return jnp.cos(ang), jnp.sin(ang)


def apply_rope(x, cos, sin):
    x1, x2 = jnp.split(x.astype(jnp.float32), 2, axis=-1)
    c = cos[None, :, None, :]
    s = sin[None, :, None, :]
    return jnp.concatenate([x1 * c - x2 * s, x2 * c + x1 * s], axis=-1).astype(x.dtype)


def swiglu(x, w_gate, w_up, w_down):
    return (jax.nn.silu(x @ w_gate) * (x @ w_up)) @ w_down


def dsa_attention(q, k, v, qi, ki, wi):
    B, S, H, D = q.shape
    topk = min(DSA_TOPK_MAX, S // 4)
    scale = D ** -0.5
    key_pos = jnp.arange(S)
    ki_f = ki.astype(jnp.float32)

    def block(i):
        q0 = i * Q_BLOCK
        qb = lax.dynamic_slice_in_dim(q, q0, Q_BLOCK, 1)
        qib = lax.dynamic_slice_in_dim(qi, q0, Q_BLOCK, 1).astype(jnp.float32)
        wib = lax.dynamic_slice_in_dim(wi, q0, Q_BLOCK, 1).astype(jnp.float32)
        qpos = q0 + jnp.arange(Q_BLOCK)
        causal = key_pos[None, :] <= qpos[:, None]
        idx_logits = jnp.einsum('bqhe,bse->bqhs', qib, ki_f)
        score = jnp.einsum('bqh,bqhs->bqs', wib * DSA_IDX_SCALE, jax.nn.relu(idx_logits))
        score = jnp.where(causal[None], score, -jnp.inf)
        _, sel = lax.top_k(score, topk)
        valid = sel <= qpos[None, :, None]
        kg = jax.vmap(lambda kk, ii: kk[ii])(k, sel)
        vg = jax.vmap(lambda vv, ii: vv[ii])(v, sel)
        s = jnp.einsum('bqhd,bqkd->bhqk', qb, kg).astype(jnp.float32) * scale
        s = jnp.where(valid[:, None], s, -jnp.inf)
        p = jax.nn.softmax(s, axis=-1).astype(v.dtype)
        return jnp.einsum('bhqk,bqkd->bqhd', p, vg)

    out = lax.map(block, jnp.arange(S // Q_BLOCK))
    return out.transpose(1, 0, 2, 3, 4).reshape(B, S, H * D)


def moba_attention(q, k, v):
    B, S, H, D = q.shape
    scale = D ** -0.5
    nblk = -(-S // MOBA_BLOCK)
    pad = nblk * MOBA_BLOCK - S
    kp = jnp.pad(k, ((0, 0), (0, pad), (0, 0), (0, 0)))
    vp = jnp.pad(v, ((0, 0), (0, pad), (0, 0), (0, 0)))
    kbh = kp.reshape(B, nblk, MOBA_BLOCK, H, D).transpose(0, 3, 1, 2, 4)
    vbh = vp.reshape(B, nblk, MOBA_BLOCK, H, D).transpose(0, 3, 1, 2, 4)
    kmean = jnp.mean(kbh.astype(jnp.float32), axis=3)
    n_sel = min(MOBA_TOPK, nblk - 1)
    blk_ids = jnp.arange(nblk)
    in_blk = jnp.arange(MOBA_BLOCK)

    def chunk(i):
        q0 = i * MOBA_Q_CHUNK
        own = q0 // MOBA_BLOCK
        qb = lax.dynamic_slice_in_dim(q, q0, MOBA_Q_CHUNK, 1)
        qpos = q0 + jnp.arange(MOBA_Q_CHUNK)
        k_own = lax.dynamic_index_in_dim(kbh, own, 2, keepdims=False)
        v_own = lax.dynamic_index_in_dim(vbh, own, 2, keepdims=False)
        own_pos = own * MOBA_BLOCK + in_blk
        s_own = jnp.einsum('bqhd,bhkd->bhqk', qb, k_own).astype(jnp.float32) * scale
        s_own = jnp.where((own_pos[None, :] <= qpos[:, None])[None, None], s_own, -jnp.inf)
        if n_sel == 0:
            p = jax.nn.softmax(s_own, axis=-1).astype(v.dtype)
            return jnp.einsum('bhqk,bhkd->bqhd', p, v_own)
        gate = jnp.einsum('bqhd,bhnd->bhqn', qb.astype(jnp.float32), kmean)
        gate = jnp.where((blk_ids < own)[None, None, None], gate, -jnp.inf)
        _, sel = lax.top_k(gate, n_sel)
        valid = sel < own
        gather = jax.vmap(jax.vmap(lambda kk, ii: kk[ii]))
        kg = gather(kbh, sel)
        vg = gather(vbh, sel)
        s_sel = jnp.einsum('bqhd,bhqnkd->bhqnk', qb, kg).astype(jnp.float32) * scale
        s_sel = jnp.where(valid[..., None], s_sel, -jnp.inf)
        s_sel = s_sel.reshape(B, H, MOBA_Q_CHUNK, n_sel * MOBA_BLOCK)
        p = jax.nn.softmax(jnp.concatenate([s_sel, s_own], axis=-1), axis=-1).astype(v.dtype)
        p_sel = p[..., :n_sel * MOBA_BLOCK].reshape(B, H, MOBA_Q_CHUNK, n_sel, MOBA_BLOCK)
        p_own = p[..., n_sel * MOBA_BLOCK:]
        return (jnp.einsum('bhqnk,bhqnkd->bqhd', p_sel, vg)
                + jnp.einsum('bhqk,bhkd->bqhd', p_own, v_own))

    out = lax.map(chunk, jnp.arange(S // MOBA_Q_CHUNK))
    return out.transpose(1, 0, 2, 3, 4).reshape(B, S, H * D)


def forgetting_attention(q, k, v, log_f):
    B, S, H, D = q.shape
    scale = D ** -0.5
    c = jnp.cumsum(log_f, axis=1).transpose(0, 2, 1)
    key_pos = jnp.arange(S)

    def block(i):
        q0 = i * Q_BLOCK
        qb = lax.dynamic_slice_in_dim(q, q0, Q_BLOCK, 1)
        cq = lax.dynamic_slice_in_dim(c, q0, Q_BLOCK, 2)
        qpos = q0 + jnp.arange(Q_BLOCK)
        s = jnp.einsum('bqhd,bkhd->bhqk', qb, k).astype(jnp.float32) * scale
        s = s + cq[..., None] - c[:, :, None, :]
        s = jnp.where((key_pos[None, :] <= qpos[:, None])[None, None], s, -jnp.inf)
        p = jax.nn.softmax(s, axis=-1).astype(v.dtype)
        return jnp.einsum('bhqk,bkhd->bqhd', p, v)

    out = lax.map(block, jnp.arange(S // Q_BLOCK))
    return out.transpose(1, 0, 2, 3, 4).reshape(B, S, H * D)


def hybrid_mixer(u, w_in, forget_bias, a_q_norm, a_k_norm, b_q_norm, b_k_norm,
                 c_q_norm, c_k_norm, w_out, cos, sin):
    B, S, _ = u.shape
    proj = u @ w_in
    splits = np.cumsum(np.array(IN_SIZES))[:-1].tolist()
    (aq, ak, av, iq, ik, iw, bq, bk, bv, cq, ck, cv, cf, cg) = jnp.split(proj, splits, axis=-1)

    aq = apply_rope(rms_norm(aq.reshape(B, S, A_HEADS, HEAD_DIM), a_q_norm), cos, sin)
    ak = apply_rope(rms_norm(ak.reshape(B, S, 1, HEAD_DIM), a_k_norm), cos, sin)[:, :, 0]
    iq = apply_rope(iq.reshape(B, S, IDX_HEADS, IDX_DIM), cos, sin)
    ik = apply_rope(ik.reshape(B, S, 1, IDX_DIM), cos, sin)[:, :, 0]
    o_a = dsa_attention(aq, ak, av, iq, ik, iw)

    bq = apply_rope(rms_norm(bq.reshape(B, S, B_HEADS, HEAD_DIM), b_q_norm), cos, sin)
    bk = apply_rope(rms_norm(bk.reshape(B, S, B_HEADS, HEAD_DIM), b_k_norm), cos, sin)
    o_b = moba_attention(bq, bk, bv.reshape(B, S, B_HEADS, HEAD_DIM))

    cq = rms_norm(cq.reshape(B, S, C_HEADS, HEAD_DIM), c_q_norm)
    ck = rms_norm(ck.reshape(B, S, C_HEADS, HEAD_DIM), c_k_norm)
    log_f = jax.nn.log_sigmoid(cf.astype(jnp.float32) + forget_bias.astype(jnp.float32))
    o_c = forgetting_attention(cq, ck, cv.reshape(B, S, C_HEADS, HEAD_DIM), log_f)
    o_c = o_c * jax.nn.sigmoid(cg)

    return jnp.concatenate([o_a, o_b, o_c], axis=-1) @ w_out


def setup_inputs(seed: int = 0) -> dict:
    key = jax.random.key(seed)
    ks = jax.random.split(key, 24)

    def w(k, shape, fan_in):
        return jax.random.normal(k, shape, jnp.float32) * (fan_in ** -0.5)

    def gain(k, shape):
        return 1.0 + 0.05 * jax.random.normal(k, shape, jnp.float32)

    return {
        "x": jax.random.normal(ks[0], (BATCH, SEQ, D_MODEL), jnp.float32),
        "ffn1_norm": gain(ks[1], (DEPTH, D_MODEL)),
        "ffn1_w_gate": w(ks[2], (DEPTH, D_MODEL, D_FF), D_MODEL),
        "ffn1_w_up": w(ks[3], (DEPTH, D_MODEL, D_FF), D_MODEL),
        "ffn1_w_down": w(ks[4], (DEPTH, D_FF, D_MODEL), D_FF),
        "mix_norm": gain(ks[5], (DEPTH, D_MODEL)),
        "w_in": w(ks[6], (DEPTH, D_MODEL, IN_WIDTH), D_MODEL),
        "forget_bias": jax.random.uniform(ks[7], (DEPTH, C_HEADS), jnp.float32, 1.0, 4.0),
        "a_q_norm": gain(ks[8], (DEPTH, HEAD_DIM)),
        "a_k_norm": gain(ks[9], (DEPTH, HEAD_DIM)),
        "b_q_norm": gain(ks[10], (DEPTH, HEAD_DIM)),
        "b_k_norm": gain(ks[11], (DEPTH, HEAD_DIM)),
        "c_q_norm": gain(ks[12], (DEPTH, HEAD_DIM)),
        "c_k_norm": gain(ks[13], (DEPTH, HEAD_DIM)),
        "w_out": w(ks[14], (DEPTH, MIX_WIDTH, D_MODEL), MIX_WIDTH),
        "ffn2_norm": gain(ks[15], (DEPTH, D_MODEL)),
        "ffn2_w_gate": w(ks[16], (DEPTH, D_MODEL, D_FF), D_MODEL),
        "ffn2_w_up": w(ks[17], (DEPTH, D_MODEL, D_FF), D_MODEL),
        "ffn2_w_down": w(ks[18], (DEPTH, D_FF, D_MODEL), D_FF),
    }


def reference(x, ffn1_norm, ffn1_w_gate, ffn1_w_up, ffn1_w_down, mix_norm, w_in, forget_bias,
              a_q_norm, a_k_norm, b_q_norm, b_k_norm, c_q_norm, c_k_norm, w_out,
              ffn2_norm, ffn2_w_gate, ffn2_w_up, ffn2_w_down):
    S = x.shape[1]
    cos, sin = rope_tables(S, HEAD_DIM)
    h = x
    for l in range(DEPTH):
        h = h + 0.5 * swiglu(rms_norm(h, ffn1_norm[l]), ffn1_w_gate[l], ffn1_w_up[l], ffn1_w_down[l])
        h = h + hybrid_mixer(rms_norm(h, mix_norm[l]), w_in[l], forget_bias[l],
                             a_q_norm[l], a_k_norm[l], b_q_norm[l], b_k_norm[l],
                             c_q_norm[l], c_k_norm[l], w_out[l], cos, sin)
        h = h + 0.5 * swiglu(rms_norm(h, ffn2_norm[l]), ffn2_w_gate[l], ffn2_w_up[l], ffn2_w_down[l])
    return h
```

```python
import numpy as np
import ml_dtypes
import concourse.bass as bass
import concourse.mybir as mybir
from concourse.bass_utils import run_bass_kernel_spmd
from contextlib import ExitStack

F32 = mybir.dt.float32
BF16 = mybir.dt.bfloat16
ALU = mybir.AluOpType
AF = mybir.ActivationFunctionType
AX = mybir.AxisListType

D = 1024
DFF = 2816
NF = DFF // 128
INW = 3792
NEG = -30000.0
NSTEP = 20
TOPK = 256
EPOCH = 30000

C_AQ, C_AK, C_AV, C_IQ, C_IK, C_IW = 0, 256, 320, 384, 896, 960
C_BQ, C_BK, C_BV = 968, 1224, 1480
C_CQ, C_CK, C_CV, C_CF, C_CG = 1736, 2248, 2760, 3272, 3280
ROT_SEGS = [(C_AQ, 256), (C_AK, 64), (C_IQ, 512), (C_IK, 64), (C_BQ, 256), (C_BK, 256)]
ROT_OFF = {}
_o = 0
for _s, _w in ROT_SEGS:
    ROT_OFF[_s] = _o
    _o += _w
ROTW = _o
V_AQ, V_AQR, V_AK, V_AKR, V_BQ, V_BQR, V_BK, V_BKR, V_CQ, V_CK = range(10)
V_N1, V_NM, V_N2, V_FB = 10, 18, 26, 34
NVEC = 35


class Buf:
    __slots__ = ("name", "w", "r", "sem", "dcount")

    def __init__(self, name):
        self.name = name
        self.w = None
        self.r = {}
        self.sem = None
        self.dcount = 0


class Eng:
    def __init__(self, name, is_pe=False):
        self.name = name
        self.ops = []
        self.seen = {}
        self.sem = None
        self.count = 0
        self.is_pe = is_pe
        self.nins = 0


class Prog:
    def __init__(self, nc):
        self.nc = nc
        self.es = ExitStack()
        self.pe = Eng("tensor", True)
        self.act = Eng("scalar")
        self.dve = Eng("vector")
        self.pool = Eng("gpsimd")
        self.sp = Eng("sync")
        self.engs = [self.pe, self.act, self.dve, self.pool, self.sp]
        self.nsem = 0
        self.dma_pool = []
        self.live_dma = []
        self.phase_es = None

    def new_sem(self, name):
        self.nsem += 1
        return self.es.enter_context(self.nc.semaphore(name))

    def uname(self, name):
        self.uid = getattr(self, "uid", 0) + 1
        return f"{name}_{self.uid}"

    def sbuf(self, name, shape, dt):
        return self.phase_es.enter_context(self.nc.sbuf_tensor(self.uname(name), shape, dt))

    def psum(self, name, shape, dt):
        return self.phase_es.enter_context(self.nc.psum_tensor(self.uname(name), shape, dt))

    def begin_phase(self):
        self.phase_es = ExitStack()

    def end_phase(self):
        self.barrier()
        for b in self.live_dma:
            if b.dcount < EPOCH - 4000:
                self.dma_pool.append((b.sem, b.dcount))
            b.sem = None
        self.live_dma = []
        self.phase_es.close()
        self.phase_es = None

    def _wait(self, eng, s, v):
        if eng.seen.get(s, 0) < v:
            eng.seen[s] = v
            eng.ops.append(("wait", s, v))

    def barrier(self):
        for e in self.engs:
            for d in self.engs:
                if d is not e and d.sem is not None and d.count > 0:
                    self._wait(e, d.sem, d.count)
            for b in self.live_dma:
                if b.sem is not None and b.dcount > 0:
                    self._wait(e, b.sem, b.dcount)

    def _deps(self, eng, reads, writes):
        deps = {}
        for b in reads:
            if b.w is not None and deps.get(b.w[0], 0) < b.w[1]:
                deps[b.w[0]] = b.w[1]
        for b in writes:
            if b.w is not None and deps.get(b.w[0], 0) < b.w[1]:
                deps[b.w[0]] = b.w[1]
            for s, v in b.r.items():
                if deps.get(s, 0) < v:
                    deps[s] = v
        for s, v in deps.items():
            if eng.is_pe and s is eng.sem:
                continue
            self._wait(eng, s, v)

    def _mark(self, tk, reads, writes):
        s, v = tk
        for b in reads:
            if b.r.get(s, 0) < v:
                b.r[s] = v
        for b in writes:
            b.w = tk
            b.r = {}

    def op(self, eng, fns, reads=(), writes=()):
        if callable(fns):
            fns = [fns]
        self._deps(eng, reads, writes)
        if eng.sem is None or eng.count >= EPOCH:
            eng.sem = self.new_sem(f"e{eng.name}{self.nsem}")
            eng.count = 0
        eng.count += 1
        tk = (eng.sem, eng.count)
        for f in fns[:-1]:
            eng.ops.append(("op", f, None, 0))
        eng.ops.append(("op", fns[-1], eng.sem, 1))
        eng.nins += len(fns)
        self._mark(tk, reads, writes)

    def dma(self, eng, fns, sb, reads=(), writes=()):
        if callable(fns):
            fns = [fns]
        if eng is self.pool:
            eng = self.act
        self._deps(eng, reads, writes)
        if sb.sem is None:
            if self.dma_pool:
                sb.sem, sb.dcount = self.dma_pool.pop()
            else:
                sb.sem, sb.dcount = self.new_sem(f"d{self.nsem}"), 0
            self.live_dma.append(sb)
        for f in fns:
            eng.ops.append(("op", f, sb.sem, 16))
        sb.dcount += 16 * len(fns)
        eng.nins += len(fns)
        self._mark((sb.sem, sb.dcount), reads, writes)

    def replay(self):
        nc = self.nc
        with nc.Block() as block:
            def run(h, eng):
                for o in eng.ops:
                    if o[0] == "wait":
                        h.wait_ge(o[1], o[2])
                    else:
                        ins = o[1](h)
                        if o[2] is not None:
                            ins.then_inc(o[2], o[3])

            @block.tensor
            def _(e):
                run(e, self.pe)

            @block.scalar
            def _(e):
                run(e, self.act)

            @block.vector
            def _(e):
                run(e, self.dve)

            @block.gpsimd
            def _(e):
                run(e, self.pool)

            @block.sync
            def _(e):
                run(e, self.sp)
        self.es.close()


class K:
    def __init__(self, S, depth, dbg=None):
        self.S = S
        self.depth = depth
        self.dbg = dbg or {}
        import os
        self.skip = set(os.environ.get('KSKIP', '').split(','))
        nc = bass.Bass("TRN2", target_bir_lowering=False)
        self.nc = nc
        self.P = Prog(nc)
        dt = nc.dram_tensor
        self.xT = dt("xT", [D, S], F32, kind="ExternalInput").ap()
        self.yT = dt("yT", [D, S], F32, kind="ExternalOutput").ap()
        self.w = []
        for l in range(depth):
            w = {}
            for nm, shp in (("f1g", [D, DFF]), ("f1u", [D, DFF]), ("f1d", [DFF, D]), ("win", [D, INW]),
                            ("wrot", [D, ROTW]), ("wout", [D, D]), ("f2g", [D, DFF]), ("f2u", [D, DFF]),
                            ("f2d", [DFF, D]), ("vecs", [128, NVEC])):
                w[nm] = dt(f"{nm}{l}", shp, F32, kind="ExternalInput").ap()
            self.w.append(w)
        self.c_cos = dt("c_cos", [128, S], F32, kind="ExternalInput").ap()
        self.c_sin = dt("c_sin", [128, S], F32, kind="ExternalInput").ap()
        self.c_bf = dt("c_bf", [128, 5 * 128 + 512], BF16, kind="ExternalInput").ap()
        self.c_f32 = dt("c_f32", [128, 256 + NSTEP + 1], F32, kind="ExternalInput").ap()
        self.c_E = dt("c_E", [32, S], BF16, kind="ExternalInput").ap()
        self.c_gm = dt("c_gm", [128, 3 * (S // 128) * 32], F32, kind="ExternalInput").ap()
        skind = "ExternalOutput" if self.dbg else "Internal"
        sc = lambda n, shp, d=BF16: dt(n, shp, d, kind=skind).ap()
        self.aqT = sc("aqT", [4, 64, S]); self.akT = sc("akT", [64, S]); self.av = sc("av", [S, 64])
        self.iqT = sc("iqT", [8, 64, S]); self.ikT = sc("ikT", [64, S]); self.iw = sc("iw", [S, 8], F32)
        self.bqT = sc("bqT", [4, 64, S]); self.bkT = sc("bkT", [4, 64, S]); self.bv = sc("bv", [S, 256])
        self.cqT = sc("cqT", [8, 67, S]); self.ckT = sc("ckT", [8, 64, S]); self.cv = sc("cv", [S, 512])
        self.cgT = sc("cgT", [512, S]); self.negc = sc("negc", [S, 8], F32)
        self.mixT = sc("mixT", [D, S])
        if self.dbg:
            self.d_score = dt("d_score", [128, S], F32, kind="ExternalOutput").ap()
            self.d_sv = dt("d_sv", [128, 8], F32, kind="ExternalOutput").ap()
            self.d_mneg = dt("d_mneg", [128, S], BF16, kind="ExternalOutput").ap()
        self.B_y = Buf("yT"); self.B_proj = Buf("proj"); self.B_mix = Buf("mix")

    def consts(self):
        P = self.P
        cb = P.sbuf("cb", [128, 5 * 128 + 512], BF16); CB = Buf("cb")
        cf = P.sbuf("cf", [128, 256 + NSTEP + 1], F32); CF = Buf("cf")
        sm = P.sbuf("sm", [128, 8], F32); SM = Buf("sm")
        P.dma(P.sp, lambda e: e.dma_start(out=cb[:], in_=self.c_bf[:, :]), CB, writes=[CB])
        P.dma(P.sp, lambda e: e.dma_start(out=cf[:], in_=self.c_f32[:, :]), CF, writes=[CF])
        P.op(P.pool, [lambda e: e.memset(sm[:, 0:1], 0.0), lambda e: e.memset(sm[:, 1:2], 1e-6),
                      lambda e: e.memset(sm[:, 2:3], 1.0), lambda e: e.memset(sm[:, 3:4], NEG)], writes=[SM])
        self.cb, self.CB, self.cf, self.CF, self.sm, self.SM = cb, CB, cf, CF, sm, SM
        self.ident = cb[:, 0:128]; self.blk64 = cb[:, 128:256]; self.onesD = cb[:, 256:384]
        self.trineg = cb[:, 384:512]; self.onesb = cb[:, 512:640]; self.ident4 = cb[:, 640:1152]
        self.ident32 = cf[:, 0:128]; self.causfill = cf[:, 128:256]; self.p2 = cf[:, 256:256 + NSTEP + 1]
        self.zero = sm[:, 0:1]; self.eps = sm[:, 1:2]; self.one = sm[:, 2:3]; self.m30k = sm[:, 3:4]

    def load_vecs(self, l):
        P = self.P
        vt = P.sbuf("vecs", [128, NVEC], F32); VT = Buf("vecs")
        P.dma(P.sp, lambda e: e.dma_start(out=vt[:], in_=self.w[l]["vecs"][:, :]), VT, writes=[VT])
        return vt, VT

    def load_weight(self, dst, DST, src, nrows_chunks, ncols, stg, STG, idx0=0):
        P = self.P
        npc = (ncols + 1407) // 1408
        pw = ncols // npc
        assert pw * npc == ncols
        n = idx0
        for c in range(nrows_chunks):
            for pc in range(npc):
                i = n % 2
                c0 = pc * pw
                P.dma(P.sp, lambda e, c=c, i=i, c0=c0: e.dma_start(out=stg[i][:, 0:pw], in_=src[c * 128:(c + 1) * 128, c0:c0 + pw]),
                      STG[i], writes=[STG[i]])
                eng = (P.dve, P.pool, P.act)[n % 3]
                if eng is P.act:
                    P.op(eng, lambda e, c=c, i=i, c0=c0: e.activation(out=dst[:, c, c0:c0 + pw], in_=stg[i][:, 0:pw], func=AF.Copy),
                         reads=[STG[i]], writes=[DST[c]])
                else:
                    P.op(eng, lambda e, c=c, i=i, c0=c0: e.tensor_copy(out=dst[:, c, c0:c0 + pw], in_=stg[i][:, 0:pw]),
                         reads=[STG[i]], writes=[DST[c]])
                n += 1
        return n

    def rmsnorm(self, H, HB, T, gcol, vt, VT, sq, SQ, ms, MS, rstd, RS, xn, XN):
        P = self.P
        P.op(P.act, lambda e: e.activation(out=sq[:, 0:8, 0:T], in_=H[:, :, 0:T], func=AF.Square), reads=[HB], writes=[SQ])
        P.op(P.pe, [(lambda e, c=c: e.matmul(ms[:, 0:T], lhsT=self.onesD, rhs=sq[:, c, 0:T], start=(c == 0), stop=(c == 7)))
                    for c in range(8)], reads=[SQ, self.CB], writes=[MS])
        P.op(P.act, lambda e: e.activation(out=rstd[:, 0:T], in_=ms[:, 0:T], func=AF.Sqrt, bias=self.eps, scale=1.0),
             reads=[MS, self.SM], writes=[RS])
        P.op(P.dve, lambda e: e.reciprocal(out=rstd[:, 0:T], in_=rstd[:, 0:T]), reads=[RS], writes=[RS])
        for c in range(8):
            P.op(P.dve, lambda e, c=c: e.scalar_tensor_tensor(out=xn[:, c, 0:T], in0=H[:, c, 0:T], scalar=vt[:, gcol + c:gcol + c + 1],
                                                            in1=rstd[:, 0:T], op0=ALU.mult, op1=ALU.mult),
                 reads=[HB, VT, RS], writes=[XN])

    def ffn_phase(self, l, which, src, with_wout):
        P, S = self.P, self.S
        T = 256
        NT = S // T
        w = self.w[l]
        P.begin_phase()
        self.consts()
        vt, VT = self.load_vecs(l)
        Wg = P.sbuf("Wg", [128, 8, DFF], BF16); WG = [Buf(f"wg{c}") for c in range(8)]
        Wu = P.sbuf("Wu", [128, 8, DFF], BF16); WU = [Buf(f"wu{c}") for c in range(8)]
        Wd = P.sbuf("Wd", [128, NF, D], BF16); WD = [Buf(f"wd{c}") for c in range(NF)]
        if with_wout:
            Wo = P.sbuf("Wo", [128, 8, D], BF16); WO = [Buf(f"wo{c}") for c in range(8)]
        es2 = ExitStack()
        stg = [es2.enter_context(self.nc.sbuf_tensor(P.uname(f"stg{i}"), [128, 1408], F32)) for i in range(2)]
        STG = [Buf(f"stg{i}") for i in range(2)]
        pre = "f1" if which == 1 else "f2"
        i0 = 0
        if with_wout:
            i0 = self.load_weight(Wo, WO, w["wout"], 8, D, stg, STG, i0)
        i0 = self.load_weight(Wg, WG, w[pre + "g"], 8, DFF, stg, STG, i0)
        i0 = self.load_weight(Wu, WU, w[pre + "u"], 8, DFF, stg, STG, i0)
        i0 = self.load_weight(Wd, WD, w[pre + "d"], NF, D, stg, STG, i0)
        P.barrier()
        es2.close()
        H = [P.sbuf(f"H{i}", [128, 8, T], F32) for i in range(2)]; HB = [Buf(f"H{i}") for i in range(2)]
        xn = P.sbuf("xn", [128, 8, T], BF16); XN = Buf("xn")
        actt = P.sbuf("actt", [128, NF, T], BF16); AC = Buf("act")
        rstd = P.sbuf("rstd", [128, T], F32); RS = Buf("rstd")
        sg = [P.sbuf(f"sg{i}", [128, T], F32) for i in range(2)]; SG = [Buf(f"sg{i}") for i in range(2)]
        psg = [P.psum(f"psg{i}", [128, 512], F32) for i in range(2)]; PSG = [Buf(f"psg{i}") for i in range(2)]
        psu = [P.psum(f"psu{i}", [128, 512], F32) for i in range(2)]; PSU = [Buf(f"psu{i}") for i in range(2)]
        psd = [P.psum(f"psd{i}", [128, 512], F32) for i in range(2)]; PSD = [Buf(f"psd{i}") for i in range(2)]
        ms = P.psum("ms", [128, 512], F32); MS = Buf("ms")
        if with_wout:
            mx = [P.sbuf("mx0", [128, 8, T], BF16)] * 2; MX = [Buf("mx0")] * 2
        gcol = V_N1 if which == 1 else V_N2
        srcv = src.rearrange("(c p) s -> p c s", p=128)
        dstv = self.yT.rearrange("(c p) s -> p c s", p=128)
        mixv = self.mixT.rearrange("(c p) s -> p c s", p=128)
        for t in range(NT):
            i = t % 2
            ts = slice(t * T, (t + 1) * T)
            Ht = H[i]
            P.dma(P.sp, lambda e, Ht=Ht, ts=ts: e.dma_start(out=Ht[:], in_=srcv[:, :, ts]), HB[i],
                  writes=[HB[i]])
            if with_wout:
                P.dma(P.sp, lambda e, i=i, ts=ts: e.dma_start(out=mx[i][:], in_=mixv[:, :, ts]), MX[i],
                      writes=[MX[i]])
                for m in range(8):
                    pb = m % 2
                    P.op(P.pe, [(lambda e, m=m, k=k, pb=pb, i=i: e.matmul(psd[pb][:, 0:T], lhsT=Wo[:, k, m * 128:(m + 1) * 128],
                                                                       rhs=mx[i][:, k, :], start=(k == 0), stop=(k == 7)))
                                for k in range(8)], reads=WO + [MX[i]], writes=[PSD[pb]])
                    P.op(P.dve, lambda e, m=m, pb=pb, Ht=Ht: e.tensor_tensor(out=Ht[:, m, :], in0=psd[pb][:, 0:T], in1=Ht[:, m, :], op=ALU.add),
                         reads=[PSD[pb], HB[i]], writes=[HB[i]])
            self.rmsnorm(Ht, HB[i], T, gcol, vt, VT, actt, AC, ms, MS, rstd, RS, xn, XN)
            for f in range(NF):
                pb = f % 2
                fs = slice(f * 128, (f + 1) * 128)
                P.op(P.pe, [(lambda e, k=k, fs=fs, pb=pb: e.matmul(psg[pb][:, 0:T], lhsT=Wg[:, k, fs], rhs=xn[:, k, :],
                                                                   start=(k == 0), stop=(k == 7))) for k in range(8)],
                     reads=WG + [XN], writes=[PSG[pb]])
                P.op(P.pe, [(lambda e, k=k, fs=fs, pb=pb: e.matmul(psu[pb][:, 0:T], lhsT=Wu[:, k, fs], rhs=xn[:, k, :],
                                                                   start=(k == 0), stop=(k == 7))) for k in range(8)],
                     reads=WU + [XN], writes=[PSU[pb]])
                P.op(P.act, lambda e, pb=pb: e.activation(out=sg[pb][:, :], in_=psg[pb][:, 0:T], func=AF.Silu),
                     reads=[PSG[pb]], writes=[SG[pb]])
                P.op(P.dve, lambda e, pb=pb, f=f: e.tensor_tensor(out=actt[:, f, :], in0=psu[pb][:, 0:T], in1=sg[pb][:, :], op=ALU.mult),
                     reads=[PSU[pb], SG[pb]], writes=[AC])
            for m in range(8):
                pb = m % 2
                P.op(P.pe, [(lambda e, m=m, f=f, pb=pb: e.matmul(psd[pb][:, 0:T], lhsT=Wd[:, f, m * 128:(m + 1) * 128], rhs=actt[:, f, :],
                                                                 start=(f == 0), stop=(f == NF - 1))) for f in range(NF)],
                     reads=WD + [AC], writes=[PSD[pb]])
                P.op(P.dve, lambda e, m=m, pb=pb, Ht=Ht: e.scalar_tensor_tensor(out=Ht[:, m, :], in0=psd[pb][:, 0:T], scalar=0.5, in1=Ht[:, m, :],
                                                                               op0=ALU.mult, op1=ALU.add),
                     reads=[PSD[pb], HB[i]], writes=[HB[i]])
            P.dma(P.pool, lambda e, Ht=Ht, ts=ts: e.dma_start(out=dstv[:, :, ts], in_=Ht[:]), HB[i],
                  reads=[HB[i]])
        P.end_phase()

    def proj_phase(self, l):
        P, S = self.P, self.S
        T = 512
        NT = S // T
        w = self.w[l]
        P.begin_phase()
        self.consts()
        vt, VT = self.load_vecs(l)
        Win = P.sbuf("Win", [128, 8, INW], BF16); WI = [Buf(f"wi{c}") for c in range(8)]
        Wr = P.sbuf("Wr", [128, 8, ROTW], BF16); WR = [Buf(f"wr{c}") for c in range(8)]
        es2 = ExitStack()
        stg = [es2.enter_context(self.nc.sbuf_tensor(P.uname(f"pstg{i}"), [128, 1408], F32)) for i in range(2)]
        STG = [Buf(f"pstg{i}") for i in range(2)]
        i0 = self.load_weight(Win, WI, w["win"], 8, INW, stg, STG, 0)
        self.load_weight(Wr, WR, w["wrot"], 8, ROTW, stg, STG, i0)
        P.barrier()
        es2.close()
        H = [P.sbuf(f"H{i}", [128, 8, T], F32) for i in range(2)]; HB = [Buf(f"H{i}") for i in range(2)]
        cs = [P.sbuf(f"cs{i}", [128, 2, T], F32) for i in range(2)]; CS = [Buf(f"cs{i}") for i in range(2)]
        u = P.sbuf("u", [128, 8, T], BF16); U = Buf("u")
        sqn = P.sbuf("sqn", [128, 8, T], BF16); SQN = Buf("sqn")
        rstd = P.sbuf("rstd", [128, T], F32); RS = Buf("rstd")
        msn = P.psum("msn", [128, 512], F32); MSN = Buf("msn")
        NSET = 2
        sqb = [P.sbuf(f"sqb{i}", [128, T], BF16) for i in range(NSET)]; SQB = [Buf(f"sqb{i}") for i in range(NSET)]
        rr = [P.sbuf(f"rr{i}", [128, T], F32) for i in range(NSET)]; RR = [Buf(f"rr{i}") for i in range(NSET)]
        t1 = [P.sbuf(f"t1{i}", [128, T], F32) for i in range(NSET)]; T1 = [Buf(f"t1{i}") for i in range(NSET)]
        t2 = [P.sbuf(f"t2{i}", [128, T], F32) for i in range(NSET)]; T2 = [Buf(f"t2{i}") for i in range(NSET)]
        ob = [P.sbuf(f"ob{i}", [128, T], BF16) for i in range(4)]; OB = [Buf(f"ob{i}") for i in range(4)]
        ps1 = [P.psum(f"ps1{i}", [128, 512], F32) for i in range(NSET)]; PS1 = [Buf(f"ps1{i}") for i in range(NSET)]
        ps2 = [P.psum(f"ps2{i}", [128, 512], F32) for i in range(NSET)]; PS2 = [Buf(f"ps2{i}") for i in range(NSET)]
        ps3 = msn; PS3 = MSN
        psv = [P.psum(f"psv{i}", [128, 512], F32) for i in range(2)]; PSV = [Buf(f"psv{i}") for i in range(2)]
        vst = [P.sbuf(f"vst{i}", [128, 832], BF16) for i in range(2)]; VST = [Buf(f"vst{i}") for i in range(2)]
        iwst = [P.sbuf(f"iwst{i}", [128, 8], F32) for i in range(2)]; IWST = [Buf(f"iwst{i}") for i in range(2)]
        ncst = [P.sbuf(f"ncst{i}", [128, 8], F32) for i in range(2)]; NCST = [Buf(f"ncst{i}") for i in range(2)]
        fe = P.sbuf("fe", [8, T], F32); FE = Buf("fe")
        cc = [P.sbuf(f"cc{i}", [8, T], F32) for i in range(2)]; CC = [Buf(f"cc{i}") for i in range(2)]
        onesr = P.sbuf("onesr", [8, T], F32); ONR = Buf("onesr")
        negb = P.sbuf("negb", [8, 1], F32); NB = Buf("negb")
        c8 = P.sbuf("c8", [8, T], F32); C8 = Buf("c8")
        hi = [P.sbuf(f"hi{j}", [8, T], BF16) for j in range(3)]; HI = [Buf(f"hi{j}") for j in range(3)]
        h32 = P.sbuf("h32", [8, T], F32); H32 = Buf("h32")
        P.op(P.pool, lambda e: e.memset(onesr[:], 1.0), writes=[ONR])
        P.op(P.pool, lambda e: e.memset(cc[1][:], 0.0), writes=[CC[1]])
        P.op(P.dve, lambda e: e.tensor_scalar(out=negb[:], in0=vt[0:8, V_FB:V_FB + 1], scalar1=-1.0, scalar2=None, op0=ALU.mult),
             reads=[VT], writes=[NB])
        yv = self.yT.rearrange("(c p) s -> p c s", p=128)
        cnt = [0]

        def fm_chunk(col0, M, rot, gi, kind, dst_fn, ts):
            s = cnt[0] % NSET
            o = cnt[0] % 4
            cnt[0] += 1
            cs_i = CS[(ts.start // T) % 2]
            cst = cs[(ts.start // T) % 2]
            P.op(P.pe, [(lambda e, k=k: e.matmul(ps1[s][0:M, :], lhsT=Win[:, k, col0:col0 + M], rhs=u[:, k, :], start=(k == 0), stop=(k == 7)))
                        for k in range(8)], reads=WI + [U], writes=[PS1[s]])
            if rot:
                r0 = ROT_OFF[rot[0]] + rot[1]
                P.op(P.pe, [(lambda e, k=k: e.matmul(ps2[s][0:M, :], lhsT=Wr[:, k, r0:r0 + M], rhs=u[:, k, :], start=(k == 0), stop=(k == 7)))
                            for k in range(8)], reads=WR + [U], writes=[PS2[s]])
            if gi is not None:
                P.op(P.act, lambda e: e.activation(out=sqb[s][0:M, :], in_=ps1[s][0:M, :], func=AF.Square), reads=[PS1[s]], writes=[SQB[s]])
                P.op(P.pe, lambda e: e.matmul(ps3[0:M, :], lhsT=self.blk64[0:M, 0:M], rhs=sqb[s][0:M, :], start=True, stop=True),
                     reads=[SQB[s], self.CB], writes=[PS3])
                P.op(P.act, lambda e: e.activation(out=rr[s][0:M, :], in_=ps3[0:M, :], func=AF.Sqrt, bias=self.eps[0:M, :], scale=1.0),
                     reads=[PS3, self.SM], writes=[RR[s]])
                P.op(P.dve, lambda e: e.reciprocal(out=rr[s][0:M, :], in_=rr[s][0:M, :]), reads=[RR[s]], writes=[RR[s]])
            if kind == "normrope":
                P.op(P.dve, lambda e: e.scalar_tensor_tensor(out=t1[s][0:M, :], in0=ps1[s][0:M, :], scalar=vt[0:M, gi:gi + 1], in1=cst[0:M, 0, :],
                                                             op0=ALU.mult, op1=ALU.mult), reads=[PS1[s], VT, cs_i], writes=[T1[s]])
                P.op(P.act, lambda e: e.activation(out=t2[s][0:M, :], in_=ps2[s][0:M, :], func=AF.Copy, scale=vt[0:M, gi + 1:gi + 2]),
                     reads=[PS2[s], VT], writes=[T2[s]])
                P.op(P.pool, lambda e: e.tensor_tensor(out=t2[s][0:M, :], in0=t2[s][0:M, :], in1=cst[0:M, 1, :], op=ALU.mult),
                     reads=[T2[s], cs_i], writes=[T2[s]])
                P.op(P.pool, lambda e: e.tensor_tensor(out=t1[s][0:M, :], in0=t1[s][0:M, :], in1=t2[s][0:M, :], op=ALU.add),
                     reads=[T1[s], T2[s]], writes=[T1[s]])
                P.op(P.pool, lambda e: e.tensor_tensor(out=ob[o][0:M, :], in0=t1[s][0:M, :], in1=rr[s][0:M, :], op=ALU.mult),
                     reads=[T1[s], RR[s]], writes=[OB[o]])
            elif kind == "rope":
                P.op(P.dve, lambda e: e.tensor_tensor(out=t1[s][0:M, :], in0=ps1[s][0:M, :], in1=cst[0:M, 0, :], op=ALU.mult),
                     reads=[PS1[s], cs_i], writes=[T1[s]])
                P.op(P.act, lambda e: e.activation(out=t2[s][0:M, :], in_=ps2[s][0:M, :], func=AF.Copy), reads=[PS2[s]], writes=[T2[s]])
                P.op(P.pool, lambda e: e.tensor_tensor(out=t2[s][0:M, :], in0=t2[s][0:M, :], in1=cst[0:M, 1, :], op=ALU.mult),
                     reads=[T2[s], cs_i], writes=[T2[s]])
                P.op(P.pool, lambda e: e.tensor_tensor(out=ob[o][0:M, :], in0=t1[s][0:M, :], in1=t2[s][0:M, :], op=ALU.add),
                     reads=[T1[s], T2[s]], writes=[OB[o]])
            elif kind == "norm":
                P.op(P.dve, lambda e: e.scalar_tensor_tensor(out=ob[o][0:M, :], in0=ps1[s][0:M, :], scalar=vt[0:M, gi:gi + 1], in1=rr[s][0:M, :],
                                                             op0=ALU.mult, op1=ALU.mult), reads=[PS1[s], VT, RR[s]], writes=[OB[o]])
            elif kind == "sigmoid":
                P.op(P.act, lambda e: e.activation(out=ob[o][0:M, :], in_=ps1[s][0:M, :], func=AF.Sigmoid), reads=[PS1[s]], writes=[OB[o]])
            fns = dst_fn(ob[o], ts)
            P.dma(P.pool, fns, OB[o], reads=[OB[o]])

        def heads2(dstT):
            def mk(c):
                def fn(obt, ts):
                    return [lambda e, hh=hh: e.dma_start(out=dstT[2 * c + hh, :, ts], in_=obt[hh * 64:(hh + 1) * 64, :]) for hh in range(2)]
                return fn
            return mk

        for t in range(NT):
            i = t % 2
            ts = slice(t * T, (t + 1) * T)
            Ht = H[i]
            P.dma(P.sp, lambda e, Ht=Ht, ts=ts: e.dma_start(out=Ht[:], in_=yv[:, :, ts]), HB[i], writes=[HB[i]])
            P.dma(P.sp, [lambda e, i=i, ts=ts: e.dma_start(out=cs[i][:, 0, :], in_=self.c_cos[:, ts]),
                         lambda e, i=i, ts=ts: e.dma_start(out=cs[i][:, 1, :], in_=self.c_sin[:, ts])], CS[i], writes=[CS[i]])
            self.rmsnorm(Ht, HB[i], T, V_NM, vt, VT, sqn, SQN, msn, MSN, rstd, RS, u, U)
            if 'fm' not in self.skip:
                for c in range(2):
                    fm_chunk(C_AQ + c * 128, 128, (C_AQ, c * 128), V_AQ, "normrope", heads2(self.aqT)(c), ts)
                fm_chunk(C_AK, 64, (C_AK, 0), V_AK, "normrope",
                         lambda obt, ts: [lambda e: e.dma_start(out=self.akT[:, ts], in_=obt[0:64, :])], ts)
                for c in range(4):
                    fm_chunk(C_IQ + c * 128, 128, (C_IQ, c * 128), None, "rope", heads2(self.iqT)(c), ts)
                fm_chunk(C_IK, 64, (C_IK, 0), None, "rope",
                         lambda obt, ts: [lambda e: e.dma_start(out=self.ikT[:, ts], in_=obt[0:64, :])], ts)
                for c in range(2):
                    fm_chunk(C_BQ + c * 128, 128, (C_BQ, c * 128), V_BQ, "normrope", heads2(self.bqT)(c), ts)
                for c in range(2):
                    fm_chunk(C_BK + c * 128, 128, (C_BK, c * 128), V_BK, "normrope", heads2(self.bkT)(c), ts)
                for c in range(4):
                    def dq(obt, ts, c=c):
                        return [lambda e, hh=hh: e.dma_start(out=self.cqT[2 * c + hh, 0:64, ts], in_=obt[hh * 64:(hh + 1) * 64, :]) for hh in range(2)]
                    fm_chunk(C_CQ + c * 128, 128, None, V_CQ, "norm", dq, ts)
                for c in range(4):
                    fm_chunk(C_CK + c * 128, 128, None, V_CK, "norm", heads2(self.ckT)(c), ts)
                for c in range(4):
                    fm_chunk(C_CG + c * 128, 128, None, None, "sigmoid",
                             lambda obt, ts, c=c: [lambda e: e.dma_start(out=self.cgT[c * 128:(c + 1) * 128, ts], in_=obt[:, :])], ts)
            if 'fg' not in self.skip:
                s = cnt[0] % NSET
                cnt[0] += 1
                P.op(P.pe, [(lambda e, k=k, s=s: e.matmul(ps1[s][0:8, :], lhsT=Win[:, k, C_CF:C_CF + 8], rhs=u[:, k, :], start=(k == 0), stop=(k == 7)))
                            for k in range(8)], reads=WI + [U], writes=[PS1[s]])
                P.op(P.act, lambda e, s=s: e.activation(out=fe[:], in_=ps1[s][0:8, :], func=AF.Exp, bias=negb[:, 0:1], scale=-1.0),
                     reads=[PS1[s], NB], writes=[FE])
                P.op(P.act, lambda e: e.activation(out=fe[:], in_=fe[:], func=AF.Ln, bias=self.one[0:8, :], scale=1.0),
                     reads=[FE, self.SM], writes=[FE])
                P.op(P.dve, lambda e: e.tensor_scalar(out=fe[:], in0=fe[:], scalar1=-1.0, scalar2=None, op0=ALU.mult), reads=[FE], writes=[FE])
                prev = cc[(t + 1) % 2]
                P.op(P.dve, lambda e, i=i, prev=prev: e.tensor_tensor_scan(out=cc[i][:], data0=onesr[:], data1=fe[:], initial=prev[:, T - 1:T],
                                                                           op0=ALU.mult, op1=ALU.add),
                     reads=[ONR, FE, CC[(t + 1) % 2]], writes=[CC[i]])
                P.op(P.dve, lambda e, i=i: e.tensor_scalar(out=c8[:], in0=cc[i][:], scalar1=8.0, scalar2=None, op0=ALU.mult), reads=[CC[i]], writes=[C8])
                for j in range(3):
                    P.op(P.dve, lambda e, j=j: e.tensor_copy(out=hi[j][:], in_=c8[:]), reads=[C8], writes=[HI[j]])
                    if j < 2:
                        P.op(P.dve, lambda e, j=j: e.tensor_copy(out=h32[:], in_=hi[j][:]), reads=[HI[j]], writes=[H32])
                        P.op(P.dve, lambda e: e.tensor_tensor(out=c8[:], in0=c8[:], in1=h32[:], op=ALU.subtract), reads=[C8, H32], writes=[C8])
                    P.dma(P.pool, lambda e, j=j, ts=ts: e.dma_start(out=self.cqT[:, 64 + j, ts], in_=hi[j][:]), HI[j],
                          reads=[HI[j]])
            if 'tm' not in self.skip:
                for q in range(4):
                    vb = (t * 4 + q) % 2
                    qs = slice(q * 128, (q + 1) * 128)
                    tok = slice(t * T + q * 128, t * T + (q + 1) * 128)
                    fl = []
                    for (c0, n, o0) in ((C_AV, 64, 0), (C_BV, 256, 64)):
                        fl += [(lambda e, k=k, c0=c0, n=n, o0=o0, qs=qs: e.matmul(psv[0][:, o0:o0 + n], lhsT=u[:, k, qs], rhs=Win[:, k, c0:c0 + n],
                                                                           start=(k == 0), stop=(k == 7))) for k in range(8)]
                    fl += [(lambda e, k=k, qs=qs: e.matmul(psv[0][:, 320:328], lhsT=u[:, k, qs], rhs=Win[:, k, C_IW:C_IW + 8],
                                                    start=(k == 0), stop=(k == 7))) for k in range(8)]
                    P.op(P.pe, fl, reads=WI + [U], writes=[PSV[0]])
                    P.op(P.pe, [(lambda e, k=k, qs=qs: e.matmul(psv[1][:, 0:512], lhsT=u[:, k, qs], rhs=Win[:, k, C_CV:C_CV + 512],
                                                         start=(k == 0), stop=(k == 7))) for k in range(8)], reads=WI + [U], writes=[PSV[1]])
                    P.op(P.act, lambda e, vb=vb: e.activation(out=vst[vb][:, 0:320], in_=psv[0][:, 0:320], func=AF.Copy), reads=[PSV[0]], writes=[VST[vb]])
                    P.op(P.dve, lambda e, vb=vb: e.tensor_copy(out=iwst[vb][:], in_=psv[0][:, 320:328]), reads=[PSV[0]], writes=[IWST[vb]])
                    P.op(P.dve, lambda e, vb=vb: e.tensor_copy(out=vst[vb][:, 320:832], in_=psv[1][:, 0:512]), reads=[PSV[1]], writes=[VST[vb]])
                    P.dma(P.pool, [lambda e, vb=vb, tok=tok: e.dma_start(out=self.av[tok, :], in_=vst[vb][:, 0:64]),
                                   lambda e, vb=vb, tok=tok: e.dma_start(out=self.bv[tok, :], in_=vst[vb][:, 64:320]),
                                   lambda e, vb=vb, tok=tok: e.dma_start(out=self.cv[tok, :], in_=vst[vb][:, 320:832])],
                          VST[vb], reads=[VST[vb]])
                    P.dma(P.pool, lambda e, vb=vb, tok=tok: e.dma_start(out=self.iw[tok, :], in_=iwst[vb][:]), IWST[vb],
                          reads=[IWST[vb]])
                    if 'fg' not in self.skip and 'nc' not in self.skip:
                        P.op(P.pe, lambda e, i=i, qs=qs: e.transpose(ps3[:, 0:8], cc[i][0:8, qs], self.ident32[0:8, 0:8]), reads=[CC[i], self.CF], writes=[PS3])
                        P.op(P.act, lambda e, vb=vb: e.activation(out=ncst[vb][:], in_=ps3[:, 0:8], func=AF.Copy, scale=-1.0), reads=[PS3], writes=[NCST[vb]])
                        P.dma(P.pool, lambda e, vb=vb, tok=tok: e.dma_start(out=self.negc[tok, :], in_=ncst[vb][:]), NCST[vb],
                              reads=[NCST[vb]])
        P.end_phase()

    def attn_phase(self, kind):
        P, S = self.P, self.S
        NT = S // 128
        VC = min(16, NT)
        NQ = S // 512
        fox = kind == "fox"
        NH = 8 if fox else 4
        KA = 67 if fox else 96
        mix0 = 512 if fox else 256
        qT = self.cqT if fox else self.bqT
        kT = self.ckT if fox else self.bkT
        vsc = self.cv if fox else self.bv
        P.begin_phase()
        self.consts()
        Qa = [P.sbuf(f"Qa{i}", [KA, S], BF16) for i in range(2)]; QA = [Buf(f"Qa{i}") for i in range(2)]
        QS = [Buf(f"Qs{i}") for i in range(2)]
        Ka = [P.sbuf(f"Ka{i}", [KA, S], BF16) for i in range(2)]; KAB = [Buf(f"Ka{i}") for i in range(2)]
        Vt = [P.sbuf(f"Vt{i}", [128, NT, 65], BF16) for i in range(2)]; VB = [Buf(f"Vt{i}") for i in range(2)]
        VO = [Buf(f"Vo{i}") for i in range(2)]
        pT = [P.sbuf(f"pT{i}", [128, 512], BF16) for i in range(4)]; PT = [Buf(f"pT{i}") for i in range(4)]
        st = [P.psum(f"st{i}", [128, 512], F32) for i in range(4)]; ST = [Buf(f"st{i}") for i in range(4)]
        oT = [P.psum(f"oT{i}", [128, 512], F32) for i in range(2)]; OT = [Buf(f"oT{i}") for i in range(2)]
        bc = P.psum("bc", [128, 512], F32); BC = Buf("bc")
        rrow = P.sbuf("rrow", [65, 512], F32); RRW = Buf("rrow")
        bcs = P.sbuf("bcs", [64, 512], F32); BCS = Buf("bcs")
        osb = [P.sbuf(f"osb{i}", [64, 512], BF16) for i in range(2)]; OSB = [Buf(f"osb{i}") for i in range(2)]
        ones32 = P.sbuf("ones32", [65, 64], F32); ON32 = Buf("ones32")
        P.op(P.pool, lambda e: e.memset(ones32[:], 1.0), writes=[ON32])
        for i in range(2):
            P.op(P.pool, lambda e, i=i: e.memset(Vt[i][:, :, 64:65], 1.0), writes=[VO[i]])
        if fox:
            NC = P.sbuf("NC", [128, NT, 8], F32); NCB = Buf("NC")
            P.dma(P.sp, lambda e: e.dma_start(out=NC[:], in_=self.negc.rearrange("(t p) h -> p t h", p=128)), NCB,
                  writes=[NCB])
            gt = [P.sbuf(f"gt{i}", [64, 512], BF16) for i in range(2)]; GT = [Buf(f"gt{i}") for i in range(2)]
            KO = [Buf(f"ko{i}") for i in range(2)]
            for i in range(2):
                P.op(P.pool, lambda e, i=i: e.memset(Ka[i][64:67, :], 1.0), writes=[KO[i]])
        else:
            KO = [Buf(f"ko{i}") for i in range(2)]
            for i in range(2):
                P.dma(P.sp, lambda e, i=i: e.dma_start(out=Ka[i][64:96, :], in_=self.c_E[:, :]), KO[i], writes=[KO[i]])
            km = P.sbuf("km", [64, 32], F32); KM = Buf("km")
            kmb = P.sbuf("kmb", [64, 32], BF16); KMB = Buf("kmb")
            G = P.sbuf("G", [128, NT * 32], F32); GB = Buf("G")
            m8 = P.sbuf("m8", [128, NT * 8], F32); M8 = Buf("m8")
            selbig = P.sbuf("selbig", [128, NT, 128], BF16); SELP = Buf("selbig")
            gm = P.sbuf("gm", [128, 3, NT * 32], F32); GM = Buf("gm")
            gps = P.psum("gps", [128, 512], F32); GPS = Buf("gps")
            P.op(P.pool, lambda e: e.memset(selbig[:], 0.0), writes=[SELP])
            P.op(P.pool, lambda e: e.memset(km[:], 0.0), writes=[KM])
            P.dma(P.sp, lambda e: e.dma_start(out=gm[:], in_=self.c_gm.rearrange("p (a n) -> p a n", a=3)), GM, writes=[GM])
        for h in range(NH):
            b = h % 2
            P.dma(P.sp, lambda e, b=b, h=h: e.dma_start(out=Qa[b][0:(67 if fox else 64), :], in_=qT[h, :, :]), QA[b],
                  writes=[QA[b]])
            P.dma(P.sp, lambda e, b=b, h=h: e.dma_start(out=Ka[b][0:64, :], in_=kT[h, :, :]), KAB[b],
                  writes=[KAB[b]])
            vv = vsc.rearrange("(t p) c -> p t c", p=128)
            P.dma(P.sp, [lambda e, b=b, h=h, t0=t0: e.dma_start(out=Vt[b][:, t0:t0 + VC, 0:64], in_=vv[:, t0:t0 + VC, h * 64:(h + 1) * 64])
                         for t0 in range(0, NT, VC)], VB[b], writes=[VB[b]])
            if not fox:
                P.op(P.dve, lambda e, b=b: e.tensor_reduce(out=km[:, 0:S // 256], in_=Ka[b][0:64, :].rearrange("p (n k) -> p n k", k=256),
                                                          axis=AX.X, op=ALU.add), reads=[KAB[b]], writes=[KM])
                P.op(P.dve, lambda e: e.tensor_copy(out=kmb[:], in_=km[:]), reads=[KM], writes=[KMB])
                for g in range(0, NT, 16):
                    ng = min(16, NT - g)
                    P.op(P.pe, [(lambda e, q=q, g=g, b=b: e.matmul(gps[:, q * 32:(q + 1) * 32], lhsT=Qa[b][0:64, (g + q) * 128:(g + q + 1) * 128],
                                                                  rhs=kmb[:, 0:32], start=True, stop=True)) for q in range(ng)],
                         reads=[QA[b], KMB], writes=[GPS])
                    P.op(P.dve, lambda e, g=g, ng=ng: e.tensor_tensor(out=G[:, g * 32:(g + ng) * 32], in0=gps[:, 0:ng * 32],
                                                                      in1=gm[:, 0, g * 32:(g + ng) * 32], op=ALU.add),
                         reads=[GPS, GM], writes=[GB])
                P.op(P.dve, [(lambda e, qt=qt: e.max(out=m8[:, qt * 8:(qt + 1) * 8], in_=G[:, qt * 32:(qt + 1) * 32])) for qt in range(NT)],
                     reads=[GB], writes=[M8])
                P.op(P.dve, [(lambda e, qt=qt: e.tensor_scalar(out=selbig[:, qt, 64:96], in0=G[:, qt * 32:(qt + 1) * 32],
                                                               scalar1=m8[:, qt * 8 + 2:qt * 8 + 3], scalar2=NEG, op0=ALU.is_lt, op1=ALU.mult))
                             for qt in range(NT)], reads=[GB, M8], writes=[SELP])
                selv = selbig[:, :, 64:96]
                P.op(P.dve, lambda e: e.tensor_tensor(out=selv, in0=selv, in1=gm[:, 1, :].rearrange("p (t n) -> p t n", n=32), op=ALU.mult),
                     reads=[SELP, GM], writes=[SELP])
                P.op(P.dve, lambda e: e.tensor_tensor(out=selv, in0=selv, in1=gm[:, 2, :].rearrange("p (t n) -> p t n", n=32), op=ALU.add),
                     reads=[SELP, GM], writes=[SELP])
                for g4 in range(0, NT, 4):
                    n4 = min(4, NT - g4)
                    P.op(P.pe, [(lambda e, q=q, g4=g4: e.matmul(gps[:, q * 128:(q + 1) * 128], lhsT=selbig[:, g4 + q, :], rhs=self.ident,
                                                                start=True, stop=True)) for q in range(n4)],
                         reads=[SELP, self.CB], writes=[GPS])
                    P.op(P.act, lambda e, b=b, g4=g4, n4=n4: e.activation(out=Qa[b][64:96, g4 * 128:(g4 + n4) * 128], in_=gps[64:96, 0:n4 * 128],
                                                                          func=AF.Copy), reads=[GPS], writes=[QS[b]])
            pairs = [(j, kt) for j in range(NQ) for kt in range(4 * j + 4)]

            def emit_qk(pi, j, kt, b=b):
                a = kt - 4 * j
                col0 = max(a, 0) * 128
                sb_ = pi % 4
                ks = slice(kt * 128, (kt + 1) * 128)
                fl = [lambda e: e.matmul(st[sb_][:, col0:512], lhsT=Ka[b][0:KA, ks], rhs=Qa[b][0:KA, j * 512 + col0:(j + 1) * 512],
                                         start=True, stop=(a < 0))]
                if a >= 0:
                    fl.append(lambda e: e.matmul(st[sb_][:, col0:col0 + 128], lhsT=self.ident, rhs=self.trineg, start=False, stop=True))
                P.op(P.pe, fl, reads=[KAB[b], KO[b], QA[b], QS[b], self.CB], writes=[ST[sb_]])

            def emit_rest(pi, j, kt, b=b, h=h):
                a = kt - 4 * j
                col0 = max(a, 0) * 128
                sb_ = pi % 4
                pb_ = pi % 4
                nk = 4 * j + 4
                ob_ = (h * NQ + j) % 2
                if kt == 0 and fox:
                    P.dma(P.sp, lambda e: e.dma_start(out=gt[ob_][:], in_=self.cgT[h * 64:(h + 1) * 64, j * 512:(j + 1) * 512]),
                          GT[ob_], writes=[GT[ob_]])
                if fox:
                    P.op(P.act, lambda e: e.activation(out=pT[pb_][:, col0:512], in_=st[sb_][:, col0:512], func=AF.Exp,
                                                       bias=NC[:, kt, h:h + 1], scale=0.125),
                         reads=[ST[sb_], NCB], writes=[PT[pb_]])
                else:
                    P.op(P.act, lambda e: e.activation(out=pT[pb_][:, col0:512], in_=st[sb_][:, col0:512], func=AF.Exp,
                                                       bias=self.zero, scale=0.125),
                         reads=[ST[sb_], self.SM], writes=[PT[pb_]])
                P.op(P.pe, lambda e: e.matmul(oT[ob_][0:65, col0:512], lhsT=Vt[b][:, kt, 0:65], rhs=pT[pb_][:, col0:512],
                                              start=(kt == 0), stop=(kt == nk - 1)),
                     reads=[VB[b], VO[b], PT[pb_]], writes=[OT[ob_]])
                if kt == nk - 1:
                    P.op(P.dve, lambda e: e.reciprocal(out=rrow[64:65, :], in_=oT[ob_][64:65, :]), reads=[OT[ob_]], writes=[RRW])
                    P.op(P.pe, lambda e: e.matmul(bc[0:64, :], lhsT=ones32[64:65, :], rhs=rrow[64:65, :], start=True, stop=True),
                         reads=[ON32, RRW], writes=[BC])
                    P.op(P.act, lambda e: e.activation(out=bcs[:], in_=bc[0:64, :], func=AF.Copy), reads=[BC], writes=[BCS])
                    if fox:
                        P.op(P.dve, lambda e: e.tensor_tensor(out=bcs[:], in0=bcs[:], in1=gt[ob_][:], op=ALU.mult),
                             reads=[BCS, GT[ob_]], writes=[BCS])
                    P.op(P.dve, lambda e: e.tensor_tensor(out=osb[ob_][:], in0=oT[ob_][0:64, :], in1=bcs[:], op=ALU.mult),
                         reads=[OT[ob_], BCS], writes=[OSB[ob_]])
                    P.dma(P.pool, lambda e: e.dma_start(out=self.mixT[mix0 + h * 64:mix0 + (h + 1) * 64, j * 512:(j + 1) * 512],
                                                        in_=osb[ob_][:]), OSB[ob_], reads=[OSB[ob_]])

            emit_qk(0, *pairs[0])
            emit_qk(1, *pairs[1])
            for pi, (j, kt) in enumerate(pairs):
                if pi + 2 < len(pairs):
                    emit_qk(pi + 2, *pairs[pi + 2])
                emit_rest(pi, j, kt)
        P.end_phase()

    def dsa_phase(self):
        P, S = self.P, self.S
        NT = S // 128
        VC = min(16, NT)
        P.begin_phase()
        self.consts()
        ik = P.sbuf("ik", [64, S], BF16); IK = Buf("ik")
        ak = P.sbuf("ak", [64, S], BF16); AK = Buf("ak")
        Vt = P.sbuf("Vt", [128, NT, 65], BF16); VB = Buf("Vt")
        P.dma(P.sp, lambda e: e.dma_start(out=ik[:], in_=self.ikT[:, :]), IK, writes=[IK])
        P.dma(P.sp, lambda e: e.dma_start(out=ak[:], in_=self.akT[:, :]), AK, writes=[AK])
        vv = self.av.rearrange("(t p) c -> p t c", p=128)
        P.op(P.pool, lambda e: e.memset(Vt[:, :, 64:65], 1.0), writes=[VB])
        P.dma(P.sp, [lambda e, t0=t0: e.dma_start(out=Vt[:, t0:t0 + VC, 0:64], in_=vv[:, t0:t0 + VC, :]) for t0 in range(0, NT, VC)],
              VB, writes=[VB])
        iq = [P.sbuf(f"iq{i}", [64, 8, 128], BF16) for i in range(2)]; IQ = [Buf(f"iq{i}") for i in range(2)]
        aq = [P.sbuf(f"aq{i}", [64, 4, 128], BF16) for i in range(2)]; AQ = [Buf(f"aq{i}") for i in range(2)]
        wq = [P.sbuf(f"wq{i}", [128, 8], F32) for i in range(2)]; WQ = [Buf(f"wq{i}") for i in range(2)]
        Dg = P.sbuf("Dg", [128, 8, 128], BF16); DG = Buf("Dg")
        R = [P.sbuf(f"R{i}", [128, 512], BF16) for i in range(3)]; RB = [Buf(f"R{i}") for i in range(3)]
        score = [P.sbuf(f"score{i}", [128, S], F32) for i in range(2)]; SC = [Buf(f"score{i}") for i in range(2)]
        mneg = [P.sbuf(f"mneg{i}", [128, S], BF16) for i in range(2)]; MN = [Buf(f"mneg{i}") for i in range(2)]
        junk = P.sbuf("junk", [128, S], BF16); JK = Buf("junk")
        junkA = P.sbuf("junkA", [128, (S * 5) // 16 + 64], BF16); JKA = Buf("junkA")
        sa = [P.sbuf(f"sa{i}", [128, 1], F32) for i in range(2)]; SA = [Buf(f"sa{i}") for i in range(2)]
        sv = P.sbuf("sv", [128, 8], F32); SV = Buf("sv")
        Hs = P.sbuf("Hs", [128, NSTEP + 1], F32); HS = Buf("Hs")
        tt = [P.sbuf(f"tt{i}", [128, 1], F32) for i in range(2)]; TT = [Buf(f"tt{i}") for i in range(2)]
        cn = [P.sbuf(f"cn{i}", [128, 1], F32) for i in range(2)]; CN = [Buf(f"cn{i}") for i in range(2)]
        uu = P.sbuf("uu", [128, 1], F32); UU = Buf("uu")
        pT = [P.sbuf(f"pT{i}", [128, 512], BF16) for i in range(3)]; PT = [Buf(f"pT{i}") for i in range(3)]
        psL = [P.psum(f"psL{i}", [128, 512], F32) for i in range(2)]; PSL = [Buf(f"psL{i}") for i in range(2)]
        psA = P.psum("psA", [128, 512], F32); PSA = Buf("psA")
        st = [P.psum(f"st{i}", [128, 512], F32) for i in range(2)]; ST = [Buf(f"st{i}") for i in range(2)]
        oT = P.psum("oT", [128, 512], F32); OT = Buf("oT")
        bc = P.psum("bc", [128, 512], F32); BC = Buf("bc")
        rrow = P.sbuf("rrow", [65, 512], F32); RRW = Buf("rrow")
        bcs = P.sbuf("bcs", [64, 512], F32); BCS = Buf("bcs")
        osb = [P.sbuf(f"osb{i}", [64, 512], BF16) for i in range(2)]; OSB = [Buf(f"osb{i}") for i in range(2)]
        ones32 = P.sbuf("ones32", [65, 64], F32); ON32 = Buf("ones32")
        P.op(P.pool, lambda e: e.memset(ones32[:], 1.0), writes=[ON32])
        iqv = self.iqT.rearrange("h d s -> d h s")
        aqv = self.aqT.rearrange("h d s -> d h s")
        mixv = self.mixT[0:256, :].rearrange("(h d) s -> d h s", d=64)
        cnt = {"r": 0, "p": 0}

        def stage_A(i):
            b = i % 2
            n = (i + 1) * 128
            qs = slice(i * 128, n)
            sc_, SCb = score[b], SC[b]
            P.dma(P.sp, lambda e: e.dma_start(out=iq[b][:], in_=iqv[:, :, qs]), IQ[b], writes=[IQ[b]])
            P.dma(P.sp, lambda e: e.dma_start(out=wq[b][:], in_=self.iw[qs, :]), WQ[b], writes=[WQ[b]])
            P.op(P.pool, [(lambda e, h=h: e.tensor_scalar(out=Dg[:, h, :], in0=self.ident, scalar1=wq[b][:, h:h + 1], scalar2=None, op0=ALU.mult))
                          for h in range(8)], reads=[WQ[b], self.CB], writes=[DG])
            nch = (n + 511) // 512
            for c in range(nch):
                ncol = min(512, n - c * 512)
                cs_ = slice(c * 512, c * 512 + ncol)
                for h in range(8):
                    lb = h % 2
                    rb = cnt["r"] % 3
                    cnt["r"] += 1
                    P.op(P.pe, lambda e, h=h, lb=lb, cs_=cs_, ncol=ncol: e.matmul(psL[lb][:, 0:ncol], lhsT=iq[b][:, h, :], rhs=ik[:, cs_],
                                                                                start=True, stop=True),
                         reads=[IQ[b], IK], writes=[PSL[lb]])
                    P.op(P.act, lambda e, lb=lb, rb=rb, ncol=ncol: e.activation(out=R[rb][:, 0:ncol], in_=psL[lb][:, 0:ncol], func=AF.Relu),
                         reads=[PSL[lb]], writes=[RB[rb]])
                    P.op(P.pe, lambda e, h=h, rb=rb, ncol=ncol: e.matmul(psA[:, 0:ncol], lhsT=Dg[:, h, :], rhs=R[rb][:, 0:ncol],
                                                                        start=(h == 0), stop=(h == 7)),
                         reads=[DG, RB[rb]], writes=[PSA])
                P.op(P.act, lambda e, cs_=cs_, ncol=ncol: e.activation(out=sc_[:, cs_], in_=psA[:, 0:ncol], func=AF.Copy),
                     reads=[PSA], writes=[SCb])

        def stage_B(i):
            b = i % 2
            n = (i + 1) * 128
            sc_, SCb = score[b], SC[b]
            P.op(P.dve, lambda e: e.tensor_reduce(out=sv[:, 0:1], in_=sc_[:, 0:n], axis=AX.X, op=ALU.max), reads=[SCb], writes=[SV])
            P.op(P.dve, lambda e: e.tensor_reduce(out=sv[:, 1:2], in_=sc_[:, 0:n], axis=AX.X, op=ALU.min), reads=[SCb], writes=[SV])
            P.op(P.dve, lambda e: e.scalar_tensor_tensor(out=sv[:, 2:3], in0=sv[:, 1:2], scalar=-1.0, in1=sv[:, 0:1], op0=ALU.mult, op1=ALU.max),
                 reads=[SV], writes=[SV])
            P.op(P.dve, lambda e: e.tensor_scalar(out=sv[:, 3:4], in0=sv[:, 2:3], scalar1=-1.001, scalar2=-1e-20, op0=ALU.mult, op1=ALU.add),
                 reads=[SV], writes=[SV])
            P.op(P.dve, lambda e: e.tensor_scalar(out=sv[:, 4:5], in0=sv[:, 3:4], scalar1=-2.0, scalar2=None, op0=ALU.mult), reads=[SV], writes=[SV])
            P.op(P.dve, lambda e: e.tensor_scalar(out=Hs[:], in0=self.p2, scalar1=sv[:, 4:5], scalar2=None, op0=ALU.mult),
                 reads=[SV, self.CF], writes=[HS])
            P.op(P.pool, lambda e: e.tensor_tensor(out=sc_[:, n - 128:n], in0=sc_[:, n - 128:n], in1=self.causfill, op=ALU.add),
                 reads=[SCb, self.CF], writes=[SCb])
            P.op(P.dve, lambda e: e.tensor_tensor(out=tt[0][:], in0=sv[:, 3:4], in1=Hs[:, 0:1], op=ALU.add), reads=[SV, HS], writes=[TT[0]])
            na = ((n * 5) // 16) // 64 * 64 if n >= 1024 else 0
            nd = n - na
            for k in range(NSTEP):
                a_, b_ = k % 2, (k + 1) % 2
                P.op(P.dve, lambda e, a_=a_: e.tensor_scalar(out=junk[:, 0:nd], in0=sc_[:, 0:nd], scalar1=tt[a_][:, 0:1], scalar2=0.0,
                                                            op0=ALU.is_ge, op1=ALU.add, accum_out=cn[a_][:, 0:1]),
                     reads=[SCb, TT[a_]], writes=[JK, CN[a_]])
                if na:
                    P.op(P.act, lambda e, a_=a_: e.activation(out=junkA[:, 0:na], in_=sc_[:, nd:n], func=AF.Sign, bias=tt[a_][:, 0:1], scale=-1.0,
                                                             accum_out=sa[a_][:, 0:1]),
                         reads=[SCb, TT[a_]], writes=[JKA, SA[a_]])
                    P.op(P.dve, lambda e, a_=a_: e.scalar_tensor_tensor(out=cn[a_][:], in0=cn[a_][:], scalar=2.0, in1=sa[a_][:],
                                                                       op0=ALU.mult, op1=ALU.subtract),
                         reads=[CN[a_], SA[a_]], writes=[CN[a_]])
                    cth = 2 * TOPK - 1.0 - na
                else:
                    cth = TOPK - 0.5
                P.op(P.dve, lambda e, k=k, a_=a_, cth=cth: e.scalar_tensor_tensor(out=uu[:], in0=cn[a_][:], scalar=cth, in1=Hs[:, k:k + 1], op0=ALU.is_ge, op1=ALU.mult),
                     reads=[CN[a_], HS], writes=[UU])
                P.op(P.dve, lambda e, k=k, a_=a_, b_=b_: e.scalar_tensor_tensor(out=tt[b_][:], in0=uu[:], scalar=Hs[:, k + 1:k + 2], in1=tt[a_][:],
                                                                               op0=ALU.subtract, op1=ALU.add),
                     reads=[UU, HS, TT[a_]], writes=[TT[b_]])
            fin = NSTEP % 2
            P.op(P.dve, lambda e: e.tensor_tensor(out=sv[:, 5:6], in0=tt[fin][:], in1=Hs[:, NSTEP:NSTEP + 1], op=ALU.subtract),
                 reads=[TT[fin], HS], writes=[SV])
            P.op(P.dve, lambda e: e.tensor_scalar(out=mneg[b][:, 0:n], in0=sc_[:, 0:n], scalar1=sv[:, 5:6], scalar2=NEG, op0=ALU.is_lt, op1=ALU.mult),
                 reads=[SCb, SV], writes=[MN[b]])
            if self.dbg and i == NT - 1:
                P.dma(P.sp, lambda e: e.dma_start(out=self.d_score[:, :], in_=sc_[:]), SCb, reads=[SCb])
                P.dma(P.sp, lambda e: e.dma_start(out=self.d_sv[:, :], in_=sv[:]), SV, reads=[SV])
                P.dma(P.sp, lambda e: e.dma_start(out=self.d_mneg[:, :], in_=mneg[b][:]), MN[b], reads=[MN[b]])

        def stage_C(i):
            b = i % 2
            n = (i + 1) * 128
            qs = slice(i * 128, n)
            P.dma(P.sp, lambda e: e.dma_start(out=aq[b][:], in_=aqv[:, :, qs]), AQ[b], writes=[AQ[b]])

            def qk(kt):
                sb_ = kt % 2
                ks = slice(kt * 128, (kt + 1) * 128)
                P.op(P.pe, [lambda e: e.matmul(st[sb_][:, 0:512], lhsT=mneg[b][:, ks], rhs=self.ident4, start=True, stop=False),
                            lambda e: e.matmul(st[sb_][:, 0:512], lhsT=ak[:, ks], rhs=aq[b][:].rearrange("d h q -> d (h q)"),
                                               start=False, stop=True)],
                     reads=[MN[b], AK, AQ[b], self.CB], writes=[ST[sb_]])
            qk(0)
            for kt in range(i + 1):
                sb_ = kt % 2
                pb_ = cnt["p"] % 3
                cnt["p"] += 1
                if kt + 1 <= i:
                    qk(kt + 1)
                P.op(P.act, lambda e, sb_=sb_, pb_=pb_: e.activation(out=pT[pb_][:], in_=st[sb_][:], func=AF.Exp, bias=self.zero, scale=0.125),
                     reads=[ST[sb_], self.SM], writes=[PT[pb_]])
                P.op(P.pe, lambda e, kt=kt, pb_=pb_: e.matmul(oT[0:65, 0:512], lhsT=Vt[:, kt, 0:65], rhs=pT[pb_][:, 0:512],
                                                             start=(kt == 0), stop=(kt == i)),
                     reads=[VB, PT[pb_]], writes=[OT])
            P.op(P.dve, lambda e: e.reciprocal(out=rrow[64:65, :], in_=oT[64:65, :]), reads=[OT], writes=[RRW])
            P.op(P.pe, lambda e: e.matmul(bc[0:64, :], lhsT=ones32[64:65, :], rhs=rrow[64:65, :], start=True, stop=True),
                 reads=[ON32, RRW], writes=[BC])
            P.op(P.act, lambda e: e.activation(out=bcs[:], in_=bc[0:64, :], func=AF.Copy), reads=[BC], writes=[BCS])
            P.op(P.dve, lambda e: e.tensor_tensor(out=osb[b][:], in0=oT[0:64, :], in1=bcs[:], op=ALU.mult), reads=[OT, BCS], writes=[OSB[b]])
            P.dma(P.pool, lambda e: e.dma_start(out=mixv[:, :, qs], in_=osb[b][:].rearrange("d (h q) -> d h q", h=4)), OSB[b],
                  reads=[OSB[b]])

        stage_A(0)
        for i in range(NT):
            if i + 1 < NT:
                stage_A(i + 1)
            stage_B(i)
            if i >= 1:
                stage_C(i - 1)
        stage_C(NT - 1)
        P.end_phase()

    def build(self, phases=None):
        P = self.P
        src = self.xT
        ph = phases or ("f1", "proj", "dsa", "moba", "fox", "f2")
        for l in range(self.depth):
            if "f1" in ph:
                self.ffn_phase(l, 1, src, False)
            src = self.yT
            if "proj" in ph:
                self.proj_phase(l)
            if "dsa" in ph:
                self.dsa_phase()
            if "moba" in ph:
                self.attn_phase("moba")
            if "fox" in ph:
                self.attn_phase("fox")
            if "f2" in ph:
                self.ffn_phase(l, 2, self.yT, True)
        P.replay()
        print("instr counts:", {e.name: (e.nins, len(e.ops)) for e in P.engs}, "sems:", P.nsem, flush=True)
        return self.nc


def _rot_perm(width):
    idx = np.arange(width)
    return (idx // 64) * 64 + ((idx % 64) + 32) % 64


def host_consts(S):
    bf = ml_dtypes.bfloat16
    inv = (1.0 / (np.float32(10000.0) ** (np.arange(0, 64, 2, dtype=np.float32) / np.float32(64)))).astype(np.float32)
    ang = (np.arange(S, dtype=np.float32)[:, None] * inv[None, :]).astype(np.float32)
    cos, sin = np.cos(ang).astype(np.float32), np.sin(ang).astype(np.float32)
    p = np.arange(128)
    c_cos = np.ascontiguousarray(cos[:, p % 32].T)
    sgn = np.where((p % 64) < 32, -1.0, 1.0).astype(np.float32)
    c_sin = np.ascontiguousarray((sin[:, p % 32] * sgn[None, :]).T).astype(np.float32)
    ident = np.eye(128, dtype=np.float32)
    blk = np.zeros((128, 128), np.float32); blk[:64, :64] = 1 / 64; blk[64:, 64:] = 1 / 64
    onesD = np.full((128, 128), 1 / 1024, np.float32)
    srow = np.arange(128)[:, None]; qcol = np.arange(128)[None, :]
    trineg = np.where(srow > qcol, NEG, 0.0).astype(np.float32)
    ones = np.ones((128, 128), np.float32)
    ident4 = np.tile(ident, (1, 4))
    c_bf = np.concatenate([ident, blk, onesD, trineg, ones, ident4], axis=1).astype(bf)
    causfill = np.where(qcol > srow, -1e30, 0.0).astype(np.float32)
    p2 = np.tile((2.0 ** -(np.arange(NSTEP + 1) + 1.0)).astype(np.float32)[None, :], (128, 1))
    c_f32 = np.concatenate([ident, causfill, p2], axis=1).astype(np.float32)
    E = (np.arange(S)[None, :] // 256 == np.arange(32)[:, None]).astype(np.float32).astype(bf)
    NT = S // 128
    own = (np.arange(NT) // 2)[:, None]
    nn = np.arange(32)[None, :]
    gfill = np.where(nn < own, 0.0, -1e30)
    m1 = np.where(nn < own, 1.0, 0.0)
    m2 = np.where(nn > own, NEG, 0.0)
    gmrow = np.concatenate([gfill.reshape(-1), m1.reshape(-1), m2.reshape(-1)]).astype(np.float32)
    c_gm = np.ascontiguousarray(np.tile(gmrow[None, :], (128, 1)))
    return {"c_cos": c_cos, "c_sin": c_sin, "c_bf": c_bf, "c_f32": c_f32, "c_E": E, "c_gm": c_gm}


def host_layer(inp, l):
    g = lambda k: np.asarray(inp[k][l], np.float32)
    win = g("w_in")
    rot_cols = np.concatenate([s + _rot_perm(w) for s, w in ROT_SEGS])
    vecs = np.zeros((128, NVEC), np.float32)
    rp = _rot_perm(64)
    for col, key in ((V_AQ, "a_q_norm"), (V_AK, "a_k_norm"), (V_BQ, "b_q_norm"), (V_BK, "b_k_norm")):
        v = g(key)
        vecs[:, col] = np.tile(v, 2)
        vecs[:, col + 1] = np.tile(v[rp], 2)
    vecs[:, V_CQ] = np.tile(g("c_q_norm"), 2)
    vecs[:, V_CK] = np.tile(g("c_k_norm"), 2)
    vecs[:, V_N1:V_N1 + 8] = g("ffn1_norm").reshape(8, 128).T
    vecs[:, V_NM:V_NM + 8] = g("mix_norm").reshape(8, 128).T
    vecs[:, V_N2:V_N2 + 8] = g("ffn2_norm").reshape(8, 128).T
    vecs[0:8, V_FB] = g("forget_bias")
    return {f"f1g{l}": g("ffn1_w_gate"), f"f1u{l}": g("ffn1_w_up"), f"f1d{l}": g("ffn1_w_down"),
            f"win{l}": win, f"wrot{l}": np.ascontiguousarray(win[:, rot_cols]), f"wout{l}": g("w_out"),
            f"f2g{l}": g("ffn2_w_gate"), f"f2u{l}": g("ffn2_w_up"), f"f2d{l}": g("ffn2_w_down"), f"vecs{l}": vecs}


def kernel(**inputs):
    x = np.asarray(inputs["x"], np.float32)
    B, S, _ = x.shape
    depth = inputs["w_in"].shape[0]
    kb = K(S, depth)
    nc = kb.build()
    shared = host_consts(S)
    for l in range(depth):
        shared.update(host_layer(inputs, l))
    in_maps = []
    for b in range(B):
        m = dict(shared)
        m["xT"] = np.ascontiguousarray(x[b].T)
        in_maps.append(m)
    res = run_bass_kernel_spmd(nc, in_maps, core_ids=list(range(B)))
    out = np.stack([np.ascontiguousarray(r["yT"].T) for r in res.results], axis=0)
    return out.astype(np.float32)
```

```python
import numpy as np
import ml_dtypes
import concourse.bass as bass
import concourse.mybir as mybir
from concourse.bass_utils import run_bass_kernel_spmd
from contextlib import ExitStack

F32 = mybir.dt.float32
BF16 = mybir.dt.bfloat16
ALU = mybir.AluOpType
AF = mybir.ActivationFunctionType
AX = mybir.AxisListType

D = 1024
DFF = 2816
NF = DFF // 128
INW = 3792
NEG = -30000.0
NSTEP = 20
TOPK = 256
EPOCH = 30000

C_AQ, C_AK, C_AV, C_IQ, C_IK, C_IW = 0, 256, 320, 384, 896, 960
C_BQ, C_BK, C_BV = 968, 1224, 1480
C_CQ, C_CK, C_CV, C_CF, C_CG = 1736, 2248, 2760, 3272, 3280
ROT_SEGS = [(C_AQ, 256), (C_AK, 64), (C_IQ, 512), (C_IK, 64), (C_BQ, 256), (C_BK, 256)]
ROT_OFF = {}
_o = 0
for _s, _w in ROT_SEGS:
    ROT_OFF[_s] = _o
    _o += _w
ROTW = _o
V_AQ, V_AQR, V_AK, V_AKR, V_BQ, V_BQR, V_BK, V_BKR, V_CQ, V_CK = range(10)
V_N1, V_NM, V_N2, V_FB = 10, 18, 26, 34
NVEC = 35


class Buf:
    __slots__ = ("name", "w", "r", "sem", "dcount")

    def __init__(self, name):
        self.name = name
        self.w = None
        self.r = {}
        self.sem = None
        self.dcount = 0


class Eng:
    def __init__(self, name, is_pe=False):
        self.name = name
        self.ops = []
        self.seen = {}
        self.sem = None
        self.count = 0
        self.is_pe = is_pe
        self.nins = 0


class Prog:
    def __init__(self, nc):
        self.nc = nc
        self.es = ExitStack()
        self.pe = Eng("tensor", True)
        self.act = Eng("scalar")
        self.dve = Eng("vector")
        self.pool = Eng("gpsimd")
        self.sp = Eng("sync")
        self.engs = [self.pe, self.act, self.dve, self.pool, self.sp]
        self.nsem = 0
        self.dma_pool = []
        self.live_dma = []
        self.phase_es = None

    def new_sem(self, name):
        self.nsem += 1
        return self.es.enter_context(self.nc.semaphore(name))

    def uname(self, name):
        self.uid = getattr(self, "uid", 0) + 1
        return f"{name}_{self.uid}"

    def sbuf(self, name, shape, dt):
        return self.phase_es.enter_context(self.nc.sbuf_tensor(self.uname(name), shape, dt))

    def psum(self, name, shape, dt):
        return self.phase_es.enter_context(self.nc.psum_tensor(self.uname(name), shape, dt))

    def begin_phase(self):
        self.phase_es = ExitStack()

    def end_phase(self):
        self.barrier()
        for b in self.live_dma:
            if b.dcount < EPOCH - 4000:
                self.dma_pool.append((b.sem, b.dcount))
            b.sem = None
        self.live_dma = []
        self.phase_es.close()
        self.phase_es = None

    def _wait(self, eng, s, v):
        if eng.seen.get(s, 0) < v:
            eng.seen[s] = v
            eng.ops.append(("wait", s, v))

    def barrier(self):
        for e in self.engs:
            for d in self.engs:
                if d is not e and d.sem is not None and d.count > 0:
                    self._wait(e, d.sem, d.count)
            for b in self.live_dma:
                if b.sem is not None and b.dcount > 0:
                    self._wait(e, b.sem, b.dcount)

    def _deps(self, eng, reads, writes):
        deps = {}
        for b in reads:
            if b.w is not None and deps.get(b.w[0], 0) < b.w[1]:
                deps[b.w[0]] = b.w[1]
        for b in writes:
            if b.w is not None and deps.get(b.w[0], 0) < b.w[1]:
                deps[b.w[0]] = b.w[1]
            for s, v in b.r.items():
                if deps.get(s, 0) < v:
                    deps[s] = v
        for s, v in deps.items():
            if eng.is_pe and s is eng.sem:
                continue
            self._wait(eng, s, v)

    def _mark(self, tk, reads, writes):
        s, v = tk
        for b in reads:
            if b.r.get(s, 0) < v:
                b.r[s] = v
        for b in writes:
            b.w = tk
            b.r = {}

    def op(self, eng, fns, reads=(), writes=()):
        if callable(fns):
            fns = [fns]
        self._deps(eng, reads, writes)
        if eng.sem is None or eng.count >= EPOCH:
            eng.sem = self.new_sem(f"e{eng.name}{self.nsem}")
            eng.count = 0
        eng.count += 1
        tk = (eng.sem, eng.count)
        for f in fns[:-1]:
            eng.ops.append(("op", f, None, 0))
        eng.ops.append(("op", fns[-1], eng.sem, 1))
        eng.nins += len(fns)
        self._mark(tk, reads, writes)

    def dma(self, eng, fns, sb, reads=(), writes=()):
        if callable(fns):
            fns = [fns]
        if eng is self.pool:
            eng = self.act
        self._deps(eng, reads, writes)
        if sb.sem is None:
            if self.dma_pool:
                sb.sem, sb.dcount = self.dma_pool.pop()
            else:
                sb.sem, sb.dcount = self.new_sem(f"d{self.nsem}"), 0
            self.live_dma.append(sb)
        for f in fns:
            eng.ops.append(("op", f, sb.sem, 16))
        sb.dcount += 16 * len(fns)
        eng.nins += len(fns)
        self._mark((sb.sem, sb.dcount), reads, writes)

    def replay(self):
        nc = self.nc
        with nc.Block() as block:
            def run(h, eng):
                for o in eng.ops:
                    if o[0] == "wait":
                        h.wait_ge(o[1], o[2])
                    else:
                        ins = o[1](h)
                        if o[2] is not None:
                            ins.then_inc(o[2], o[3])

            @block.tensor
            def _(e):
                run(e, self.pe)

            @block.scalar
            def _(e):
                run(e, self.act)

            @block.vector
            def _(e):
                run(e, self.dve)

            @block.gpsimd
            def _(e):
                run(e, self.pool)

            @block.sync
            def _(e):
                run(e, self.sp)
        self.es.close()


class K:
    def __init__(self, S, depth, dbg=None):
        self.S = S
        self.depth = depth
        self.dbg = dbg or {}
        import os
        self.skip = set(os.environ.get('KSKIP', '').split(','))
        nc = bass.Bass("TRN2", target_bir_lowering=False)
        self.nc = nc
        self.P = Prog(nc)
        dt = nc.dram_tensor
        self.xT = dt("xT", [D, S], F32, kind="ExternalInput").ap()
        self.yT = dt("yT", [D, S], F32, kind="ExternalOutput").ap()
        self.w = []
        for l in range(depth):
            w = {}
            for nm, shp in (("f1g", [D, DFF]), ("f1u", [D, DFF]), ("f1d", [DFF, D]), ("win", [D, INW]),
                            ("wrot", [D, ROTW]), ("wout", [D, D]), ("f2g", [D, DFF]), ("f2u", [D, DFF]),
                            ("f2d", [DFF, D]), ("vecs", [128, NVEC])):
                w[nm] = dt(f"{nm}{l}", shp, F32, kind="ExternalInput").ap()
            self.w.append(w)
        self.c_cos = dt("c_cos", [128, S], F32, kind="ExternalInput").ap()
        self.c_sin = dt("c_sin", [128, S], F32, kind="ExternalInput").ap()
        self.c_bf = dt("c_bf", [128, 5 * 128 + 512], BF16, kind="ExternalInput").ap()
        self.c_f32 = dt("c_f32", [128, 256 + NSTEP + 1], F32, kind="ExternalInput").ap()
        self.c_E = dt("c_E", [32, S], BF16, kind="ExternalInput").ap()
        self.c_gm = dt("c_gm", [128, 3 * (S // 128) * 32], F32, kind="ExternalInput").ap()
        skind = "ExternalOutput" if self.dbg else "Internal"
        sc = lambda n, shp, d=BF16: dt(n, shp, d, kind=skind).ap()
        self.aqT = sc("aqT", [4, 64, S]); self.akT = sc("akT", [64, S]); self.av = sc("av", [S, 64])
        self.iqT = sc("iqT", [8, 64, S]); self.ikT = sc("ikT", [64, S]); self.iw = sc("iw", [S, 8], F32)
        self.bqT = sc("bqT", [4, 64, S]); self.bkT = sc("bkT", [4, 64, S]); self.bv = sc("bv", [S, 256])
        self.cqT = sc("cqT", [8, 67, S]); self.ckT = sc("ckT", [8, 64, S]); self.cv = sc("cv", [S, 512])
        self.cgT = sc("cgT", [512, S]); self.negc = sc("negc", [S, 8], F32)
        self.mixT = sc("mixT", [D, S])
        if self.dbg:
            self.d_score = dt("d_score", [128, S], F32, kind="ExternalOutput").ap()
            self.d_sv = dt("d_sv", [128, 8], F32, kind="ExternalOutput").ap()
            self.d_mneg = dt("d_mneg", [128, S], BF16, kind="ExternalOutput").ap()
        self.B_y = Buf("yT"); self.B_proj = Buf("proj"); self.B_mix = Buf("mix")

    def consts(self):
        P = self.P
        cb = P.sbuf("cb", [128, 5 * 128 + 512], BF16); CB = Buf("cb")
        cf = P.sbuf("cf", [128, 256 + NSTEP + 1], F32); CF = Buf("cf")
        sm = P.sbuf("sm", [128, 8], F32); SM = Buf("sm")
        P.dma(P.sp, lambda e: e.dma_start(out=cb[:], in_=self.c_bf[:, :]), CB, writes=[CB])
        P.dma(P.sp, lambda e: e.dma_start(out=cf[:], in_=self.c_f32[:, :]), CF, writes=[CF])
        P.op(P.pool, [lambda e: e.memset(sm[:, 0:1], 0.0), lambda e: e.memset(sm[:, 1:2], 1e-6),
                      lambda e: e.memset(sm[:, 2:3], 1.0), lambda e: e.memset(sm[:, 3:4], NEG)], writes=[SM])
        self.cb, self.CB, self.cf, self.CF, self.sm, self.SM = cb, CB, cf, CF, sm, SM
        self.ident = cb[:, 0:128]; self.blk64 = cb[:, 128:256]; self.onesD = cb[:, 256:384]
        self.trineg = cb[:, 384:512]; self.onesb = cb[:, 512:640]; self.ident4 = cb[:, 640:1152]
        self.ident32 = cf[:, 0:128]; self.causfill = cf[:, 128:256]; self.p2 = cf[:, 256:256 + NSTEP + 1]
        self.zero = sm[:, 0:1]; self.eps = sm[:, 1:2]; self.one = sm[:, 2:3]; self.m30k = sm[:, 3:4]

    def load_vecs(self, l):
        P = self.P
        vt = P.sbuf("vecs", [128, NVEC], F32); VT = Buf("vecs")
        P.dma(P.sp, lambda e: e.dma_start(out=vt[:], in_=self.w[l]["vecs"][:, :]), VT, writes=[VT])
        return vt, VT

    def load_weight(self, dst, DST, src, nrows_chunks, ncols, stg, STG, idx0=0):
        P = self.P
        npc = (ncols + 1407) // 1408
        pw = ncols // npc
        assert pw * npc == ncols
        n = idx0
        for c in range(nrows_chunks):
            for pc in range(npc):
                i = n % 2
                c0 = pc * pw
                P.dma(P.sp, lambda e, c=c, i=i, c0=c0: e.dma_start(out=stg[i][:, 0:pw], in_=src[c * 128:(c + 1) * 128, c0:c0 + pw]),
                      STG[i], writes=[STG[i]])
                eng = (P.dve, P.pool, P.act)[n % 3]
                if eng is P.act:
                    P.op(eng, lambda e, c=c, i=i, c0=c0: e.activation(out=dst[:, c, c0:c0 + pw], in_=stg[i][:, 0:pw], func=AF.Copy),
                         reads=[STG[i]], writes=[DST[c]])
                else:
                    P.op(eng, lambda e, c=c, i=i, c0=c0: e.tensor_copy(out=dst[:, c, c0:c0 + pw], in_=stg[i][:, 0:pw]),
                         reads=[STG[i]], writes=[DST[c]])
                n += 1
        return n

    def rmsnorm(self, H, HB, T, gcol, vt, VT, sq, SQ, ms, MS, rstd, RS, xn, XN):
        P = self.P
        P.op(P.act, lambda e: e.activation(out=sq[:, 0:8, 0:T], in_=H[:, :, 0:T], func=AF.Square), reads=[HB], writes=[SQ])
        P.op(P.pe, [(lambda e, c=c: e.matmul(ms[:, 0:T], lhsT=self.onesD, rhs=sq[:, c, 0:T], start=(c == 0), stop=(c == 7)))
                    for c in range(8)], reads=[SQ, self.CB], writes=[MS])
        P.op(P.act, lambda e: e.activation(out=rstd[:, 0:T], in_=ms[:, 0:T], func=AF.Sqrt, bias=self.eps, scale=1.0),
             reads=[MS, self.SM], writes=[RS])
        P.op(P.dve, lambda e: e.reciprocal(out=rstd[:, 0:T], in_=rstd[:, 0:T]), reads=[RS], writes=[RS])
        for c in range(8):
            P.op(P.dve, lambda e, c=c: e.scalar_tensor_tensor(out=xn[:, c, 0:T], in0=H[:, c, 0:T], scalar=vt[:, gcol + c:gcol + c + 1],
                                                            in1=rstd[:, 0:T], op0=ALU.mult, op1=ALU.mult),
                 reads=[HB, VT, RS], writes=[XN])

    def ffn_phase(self, l, which, src, with_wout):
        P, S = self.P, self.S
        T = 256
        NT = S // T
        w = self.w[l]
        P.begin_phase()
        self.consts()
        vt, VT = self.load_vecs(l)
        Wg = P.sbuf("Wg", [128, 8, DFF], BF16); WG = [Buf(f"wg{c}") for c in range(8)]
        Wu = P.sbuf("Wu", [128, 8, DFF], BF16); WU = [Buf(f"wu{c}") for c in range(8)]
        Wd = P.sbuf("Wd", [128, NF, D], BF16); WD = [Buf(f"wd{c}") for c in range(NF)]
        if with_wout:
            Wo = P.sbuf("Wo", [128, 8, D], BF16); WO = [Buf(f"wo{c}") for c in range(8)]
        es2 = ExitStack()
        stg = [es2.enter_context(self.nc.sbuf_tensor(P.uname(f"stg{i}"), [128, 1408], F32)) for i in range(2)]
        STG = [Buf(f"stg{i}") for i in range(2)]
        pre = "f1" if which == 1 else "f2"
        i0 = 0
        if with_wout:
            i0 = self.load_weight(Wo, WO, w["wout"], 8, D, stg, STG, i0)
        i0 = self.load_weight(Wg, WG, w[pre + "g"], 8, DFF, stg, STG, i0)
        i0 = self.load_weight(Wu, WU, w[pre + "u"], 8, DFF, stg, STG, i0)
        i0 = self.load_weight(Wd, WD, w[pre + "d"], NF, D, stg, STG, i0)
        P.barrier()
        es2.close()
        H = [P.sbuf(f"H{i}", [128, 8, T], F32) for i in range(2)]; HB = [Buf(f"H{i}") for i in range(2)]
        xn = P.sbuf("xn", [128, 8, T], BF16); XN = Buf("xn")
        actt = P.sbuf("actt", [128, NF, T], BF16); AC = Buf("act")
        rstd = P.sbuf("rstd", [128, T], F32); RS = Buf("rstd")
        sg = [P.sbuf(f"sg{i}", [128, T], F32) for i in range(2)]; SG = [Buf(f"sg{i}") for i in range(2)]
        psg = [P.psum(f"psg{i}", [128, 512], F32) for i in range(2)]; PSG = [Buf(f"psg{i}") for i in range(2)]
        psu = [P.psum(f"psu{i}", [128, 512], F32) for i in range(2)]; PSU = [Buf(f"psu{i}") for i in range(2)]
        psd = [P.psum(f"psd{i}", [128, 512], F32) for i in range(2)]; PSD = [Buf(f"psd{i}") for i in range(2)]
        ms = P.psum("ms", [128, 512], F32); MS = Buf("ms")
        if with_wout:
            mx = [P.sbuf("mx0", [128, 8, T], BF16)] * 2; MX = [Buf("mx0")] * 2
        gcol = V_N1 if which == 1 else V_N2
        srcv = src.rearrange("(c p) s -> p c s", p=128)
        dstv = self.yT.rearrange("(c p) s -> p c s", p=128)
        mixv = self.mixT.rearrange("(c p) s -> p c s", p=128)
        for t in range(NT):
            i = t % 2
            ts = slice(t * T, (t + 1) * T)
            Ht = H[i]
            P.dma(P.sp, lambda e, Ht=Ht, ts=ts: e.dma_start(out=Ht[:], in_=srcv[:, :, ts]), HB[i],
                  writes=[HB[i]])
            if with_wout:
                P.dma(P.sp, lambda e, i=i, ts=ts: e.dma_start(out=mx[i][:], in_=mixv[:, :, ts]), MX[i],
                      writes=[MX[i]])
                for m in range(8):
                    pb = m % 2
                    P.op(P.pe, [(lambda e, m=m, k=k, pb=pb, i=i: e.matmul(psd[pb][:, 0:T], lhsT=Wo[:, k, m * 128:(m + 1) * 128],
                                                                       rhs=mx[i][:, k, :], start=(k == 0), stop=(k == 7)))
                                for k in range(8)], reads=WO + [MX[i]], writes=[PSD[pb]])
                    P.op(P.dve, lambda e, m=m, pb=pb, Ht=Ht: e.tensor_tensor(out=Ht[:, m, :], in0=psd[pb][:, 0:T], in1=Ht[:, m, :], op=ALU.add),
                         reads=[PSD[pb], HB[i]], writes=[HB[i]])
            self.rmsnorm(Ht, HB[i], T, gcol, vt, VT, actt, AC, ms, MS, rstd, RS, xn, XN)
            for f in range(NF):
                pb = f % 2
                fs = slice(f * 128, (f + 1) * 128)
                P.op(P.pe, [(lambda e, k=k, fs=fs, pb=pb: e.matmul(psg[pb][:, 0:T], lhsT=Wg[:, k, fs], rhs=xn[:, k, :],
                                                                   start=(k == 0), stop=(k == 7))) for k in range(8)],
                     reads=WG + [XN], writes=[PSG[pb]])
                P.op(P.pe, [(lambda e, k=k, fs=fs, pb=pb: e.matmul(psu[pb][:, 0:T], lhsT=Wu[:, k, fs], rhs=xn[:, k, :],
                                                                   start=(k == 0), stop=(k == 7))) for k in range(8)],
                     reads=WU + [XN], writes=[PSU[pb]])
                P.op(P.act, lambda e, pb=pb: e.activation(out=sg[pb][:, :], in_=psg[pb][:, 0:T], func=AF.Silu),
                     reads=[PSG[pb]], writes=[SG[pb]])
                P.op(P.dve, lambda e, pb=pb, f=f: e.tensor_tensor(out=actt[:, f, :], in0=psu[pb][:, 0:T], in1=sg[pb][:, :], op=ALU.mult),
                     reads=[PSU[pb], SG[pb]], writes=[AC])
            for m in range(8):
                pb = m % 2
                P.op(P.pe, [(lambda e, m=m, f=f, pb=pb: e.matmul(psd[pb][:, 0:T], lhsT=Wd[:, f, m * 128:(m + 1) * 128], rhs=actt[:, f, :],
                                                                 start=(f == 0), stop=(f == NF - 1))) for f in range(NF)],
                     reads=WD + [AC], writes=[PSD[pb]])
                P.op(P.dve, lambda e, m=m, pb=pb, Ht=Ht: e.scalar_tensor_tensor(out=Ht[:, m, :], in0=psd[pb][:, 0:T], scalar=0.5, in1=Ht[:, m, :],
                                                                               op0=ALU.mult, op1=ALU.add),
                     reads=[PSD[pb], HB[i]], writes=[HB[i]])
            P.dma(P.pool, lambda e, Ht=Ht, ts=ts: e.dma_start(out=dstv[:, :, ts], in_=Ht[:]), HB[i],
                  reads=[HB[i]])
        P.end_phase()

    def proj_phase(self, l):
        P, S = self.P, self.S
        T = 512
        NT = S // T
        w = self.w[l]
        P.begin_phase()
        self.consts()
        vt, VT = self.load_vecs(l)
        Win = P.sbuf("Win", [128, 8, INW], BF16); WI = [Buf(f"wi{c}") for c in range(8)]
        Wr = P.sbuf("Wr", [128, 8, ROTW], BF16); WR = [Buf(f"wr{c}") for c in range(8)]
        es2 = ExitStack()
        stg = [es2.enter_context(self.nc.sbuf_tensor(P.uname(f"pstg{i}"), [128, 1408], F32)) for i in range(2)]
        STG = [Buf(f"pstg{i}") for i in range(2)]
        i0 = self.load_weight(Win, WI, w["win"], 8, INW, stg, STG, 0)
        self.load_weight(Wr, WR, w["wrot"], 8, ROTW, stg, STG, i0)
        P.barrier()
        es2.close()
        H = [P.sbuf(f"H{i}", [128, 8, T], F32) for i in range(2)]; HB = [Buf(f"H{i}") for i in range(2)]
        cs = [P.sbuf(f"cs{i}", [128, 2, T], F32) for i in range(2)]; CS = [Buf(f"cs{i}") for i in range(2)]
        u = P.sbuf("u", [128, 8, T], BF16); U = Buf("u")
        sqn = P.sbuf("sqn", [128, 8, T], BF16); SQN = Buf("sqn")
        rstd = P.sbuf("rstd", [128, T], F32); RS = Buf("rstd")
        msn = P.psum("msn", [128, 512], F32); MSN = Buf("msn")
        NSET = 2
        sqb = [P.sbuf(f"sqb{i}", [128, T], BF16) for i in range(NSET)]; SQB = [Buf(f"sqb{i}") for i in range(NSET)]
        rr = [P.sbuf(f"rr{i}", [128, T], F32) for i in range(NSET)]; RR = [Buf(f"rr{i}") for i in range(NSET)]
        t1 = [P.sbuf(f"t1{i}", [128, T], F32) for i in range(NSET)]; T1 = [Buf(f"t1{i}") for i in range(NSET)]
        t2 = [P.sbuf(f"t2{i}", [128, T], F32) for i in range(NSET)]; T2 = [Buf(f"t2{i}") for i in range(NSET)]
        ob = [P.sbuf(f"ob{i}", [128, T], BF16) for i in range(4)]; OB = [Buf(f"ob{i}") for i in range(4)]
        ps1 = [P.psum(f"ps1{i}", [128, 512], F32) for i in range(NSET)]; PS1 = [Buf(f"ps1{i}") for i in range(NSET)]
        ps2 = [P.psum(f"ps2{i}", [128, 512], F32) for i in range(NSET)]; PS2 = [Buf(f"ps2{i}") for i in range(NSET)]
        ps3 = msn; PS3 = MSN
        psv = [P.psum(f"psv{i}", [128, 512], F32) for i in range(2)]; PSV = [Buf(f"psv{i}") for i in range(2)]
        vst = [P.sbuf(f"vst{i}", [128, 832], BF16) for i in range(2)]; VST = [Buf(f"vst{i}") for i in range(2)]
        iwst = [P.sbuf(f"iwst{i}", [128, 8], F32) for i in range(2)]; IWST = [Buf(f"iwst{i}") for i in range(2)]
        ncst = [P.sbuf(f"ncst{i}", [128, 8], F32) for i in range(2)]; NCST = [Buf(f"ncst{i}") for i in range(2)]
        fe = P.sbuf("fe", [8, T], F32); FE = Buf("fe")
        cc = [P.sbuf(f"cc{i}", [8, T], F32) for i in range(2)]; CC = [Buf(f"cc{i}") for i in range(2)]
        onesr = P.sbuf("onesr", [8, T], F32); ONR = Buf("onesr")
        negb = P.sbuf("negb", [8, 1], F32); NB = Buf("negb")
        c8 = P.sbuf("c8", [8, T], F32); C8 = Buf("c8")
        hi = [P.sbuf(f"hi{j}", [8, T], BF16) for j in range(3)]; HI = [Buf(f"hi{j}") for j in range(3)]
        h32 = P.sbuf("h32", [8, T], F32); H32 = Buf("h32")
        P.op(P.pool, lambda e: e.memset(onesr[:], 1.0), writes=[ONR])
        P.op(P.pool, lambda e: e.memset(cc[1][:], 0.0), writes=[CC[1]])
        P.op(P.dve, lambda e: e.tensor_scalar(out=negb[:], in0=vt[0:8, V_FB:V_FB + 1], scalar1=-1.0, scalar2=None, op0=ALU.mult),
             reads=[VT], writes=[NB])
        yv = self.yT.rearrange("(c p) s -> p c s", p=128)
        cnt = [0]

        def fm_chunk(col0, M, rot, gi, kind, dst_fn, ts):
            s = cnt[0] % NSET
            o = cnt[0] % 4
            cnt[0] += 1
            cs_i = CS[(ts.start // T) % 2]
            cst = cs[(ts.start // T) % 2]
            P.op(P.pe, [(lambda e, k=k: e.matmul(ps1[s][0:M, :], lhsT=Win[:, k, col0:col0 + M], rhs=u[:, k, :], start=(k == 0), stop=(k == 7)))
                        for k in range(8)], reads=WI + [U], writes=[PS1[s]])
            if rot:
                r0 = ROT_OFF[rot[0]] + rot[1]
                P.op(P.pe, [(lambda e, k=k: e.matmul(ps2[s][0:M, :], lhsT=Wr[:, k, r0:r0 + M], rhs=u[:, k, :], start=(k == 0), stop=(k == 7)))
                            for k in range(8)], reads=WR + [U], writes=[PS2[s]])
            if gi is not None:
                P.op(P.act, lambda e: e.activation(out=sqb[s][0:M, :], in_=ps1[s][0:M, :], func=AF.Square), reads=[PS1[s]], writes=[SQB[s]])
                P.op(P.pe, lambda e: e.matmul(ps3[0:M, :], lhsT=self.blk64[0:M, 0:M], rhs=sqb[s][0:M, :], start=True, stop=True),
                     reads=[SQB[s], self.CB], writes=[PS3])
                P.op(P.act, lambda e: e.activation(out=rr[s][0:M, :], in_=ps3[0:M, :], func=AF.Sqrt, bias=self.eps[0:M, :], scale=1.0),
                     reads=[PS3, self.SM], writes=[RR[s]])
                P.op(P.dve, lambda e: e.reciprocal(out=rr[s][0:M, :], in_=rr[s][0:M, :]), reads=[RR[s]], writes=[RR[s]])
            if kind == "normrope":
                P.op(P.dve, lambda e: e.scalar_tensor_tensor(out=t1[s][0:M, :], in0=ps1[s][0:M, :], scalar=vt[0:M, gi:gi + 1], in1=cst[0:M, 0, :],
                                                             op0=ALU.mult, op1=ALU.mult), reads=[PS1[s], VT, cs_i], writes=[T1[s]])
                P.op(P.act, lambda e: e.activation(out=t2[s][0:M, :], in_=ps2[s][0:M, :], func=AF.Copy, scale=vt[0:M, gi + 1:gi + 2]),
                     reads=[PS2[s], VT], writes=[T2[s]])
                P.op(P.pool, lambda e: e.tensor_tensor(out=t2[s][0:M, :], in0=t2[s][0:M, :], in1=cst[0:M, 1, :], op=ALU.mult),
                     reads=[T2[s], cs_i], writes=[T2[s]])
                P.op(P.pool, lambda e: e.tensor_tensor(out=t1[s][0:M, :], in0=t1[s][0:M, :], in1=t2[s][0:M, :], op=ALU.add),
                     reads=[T1[s], T2[s]], writes=[T1[s]])
                P.op(P.pool, lambda e: e.tensor_tensor(out=ob[o][0:M, :], in0=t1[s][0:M, :], in1=rr[s][0:M, :], op=ALU.mult),
                     reads=[T1[s], RR[s]], writes=[OB[o]])
            elif kind == "rope":
                P.op(P.dve, lambda e: e.tensor_tensor(out=t1[s][0:M, :], in0=ps1[s][0:M, :], in1=cst[0:M, 0, :], op=ALU.mult),
                     reads=[PS1[s], cs_i], writes=[T1[s]])
                P.op(P.act, lambda e: e.activation(out=t2[s][0:M, :], in_=ps2[s][0:M, :], func=AF.Copy), reads=[PS2[s]], writes=[T2[s]])
                P.op(P.pool, lambda e: e.tensor_tensor(out=t2[s][0:M, :], in0=t2[s][0:M, :], in1=cst[0:M, 1, :], op=ALU.mult),
                     reads=[T2[s], cs_i], writes=[T2[s]])
                P.op(P.pool, lambda e: e.tensor_tensor(out=ob[o][0:M, :], in0=t1[s][0:M, :], in1=t2[s][0:M, :], op=ALU.add),
                     reads=[T1[s], T2[s]], writes=[OB[o]])
            elif kind == "norm":
                P.op(P.dve, lambda e: e.scalar_tensor_tensor(out=ob[o][0:M, :], in0=ps1[s][0:M, :], scalar=vt[0:M, gi:gi + 1], in1=rr[s][0:M, :],
                                                             op0=ALU.mult, op1=ALU.mult), reads=[PS1[s], VT, RR[s]], writes=[OB[o]])
            elif kind == "sigmoid":
                P.op(P.act, lambda e: e.activation(out=ob[o][0:M, :], in_=ps1[s][0:M, :], func=AF.Sigmoid), reads=[PS1[s]], writes=[OB[o]])
            fns = dst_fn(ob[o], ts)
            P.dma(P.pool, fns, OB[o], reads=[OB[o]])

        def heads2(dstT):
            def mk(c):
                def fn(obt, ts):
                    return [lambda e, hh=hh: e.dma_start(out=dstT[2 * c + hh, :, ts], in_=obt[hh * 64:(hh + 1) * 64, :]) for hh in range(2)]
                return fn
            return mk

        for t in range(NT):
            i = t % 2
            ts = slice(t * T, (t + 1) * T)
            Ht = H[i]
            P.dma(P.sp, lambda e, Ht=Ht, ts=ts: e.dma_start(out=Ht[:], in_=yv[:, :, ts]), HB[i], writes=[HB[i]])
            P.dma(P.sp, [lambda e, i=i, ts=ts: e.dma_start(out=cs[i][:, 0, :], in_=self.c_cos[:, ts]),
                         lambda e, i=i, ts=ts: e.dma_start(out=cs[i][:, 1, :], in_=self.c_sin[:, ts])], CS[i], writes=[CS[i]])
            self.rmsnorm(Ht, HB[i], T, V_NM, vt, VT, sqn, SQN, msn, MSN, rstd, RS, u, U)
            if 'fm' not in self.skip:
                for c in range(2):
                    fm_chunk(C_AQ + c * 128, 128, (C_AQ, c * 128), V_AQ, "normrope", heads2(self.aqT)(c), ts)
                fm_chunk(C_AK, 64, (C_AK, 0), V_AK, "normrope",
                         lambda obt, ts: [lambda e: e.dma_start(out=self.akT[:, ts], in_=obt[0:64, :])], ts)
                for c in range(4):
                    fm_chunk(C_IQ + c * 128, 128, (C_IQ, c * 128), None, "rope", heads2(self.iqT)(c), ts)
                fm_chunk(C_IK, 64, (C_IK, 0), None, "rope",
                         lambda obt, ts: [lambda e: e.dma_start(out=self.ikT[:, ts], in_=obt[0:64, :])], ts)
                for c in range(2):
                    fm_chunk(C_BQ + c * 128, 128, (C_BQ, c * 128), V_BQ, "normrope", heads2(self.bqT)(c), ts)
                for c in range(2):
                    fm_chunk(C_BK + c * 128, 128, (C_BK, c * 128), V_BK, "normrope", heads2(self.bkT)(c), ts)
                for c in range(4):
                    def dq(obt, ts, c=c):
                        return [lambda e, hh=hh: e.dma_start(out=self.cqT[2 * c + hh, 0:64, ts], in_=obt[hh * 64:(hh + 1) * 64, :]) for hh in range(2)]
                    fm_chunk(C_CQ + c * 128, 128, None, V_CQ, "norm", dq, ts)
                for c in range(4):
                    fm_chunk(C_CK + c * 128, 128, None, V_CK, "norm", heads2(self.ckT)(c), ts)
                for c in range(4):
                    fm_chunk(C_CG + c * 128, 128, None, None, "sigmoid",
                             lambda obt, ts, c=c: [lambda e: e.dma_start(out=self.cgT[c * 128:(c + 1) * 128, ts], in_=obt[:, :])], ts)
            if 'fg' not in self.skip:
                s = cnt[0] % NSET
                cnt[0] += 1
                P.op(P.pe, [(lambda e, k=k, s=s: e.matmul(ps1[s][0:8, :], lhsT=Win[:, k, C_CF:C_CF + 8], rhs=u[:, k, :], start=(k == 0), stop=(k == 7)))
                            for k in range(8)], reads=WI + [U], writes=[PS1[s]])
                P.op(P.act, lambda e, s=s: e.activation(out=fe[:], in_=ps1[s][0:8, :], func=AF.Exp, bias=negb[:, 0:1], scale=-1.0),
                     reads=[PS1[s], NB], writes=[FE])
                P.op(P.act, lambda e: e.activation(out=fe[:], in_=fe[:], func=AF.Ln, bias=self.one[0:8, :], scale=1.0),
                     reads=[FE, self.SM], writes=[FE])
                P.op(P.dve, lambda e: e.tensor_scalar(out=fe[:], in0=fe[:], scalar1=-1.0, scalar2=None, op0=ALU.mult), reads=[FE], writes=[FE])
                prev = cc[(t + 1) % 2]
                P.op(P.dve, lambda e, i=i, prev=prev: e.tensor_tensor_scan(out=cc[i][:], data0=onesr[:], data1=fe[:], initial=prev[:, T - 1:T],
                                                                           op0=ALU.mult, op1=ALU.add),
                     reads=[ONR, FE, CC[(t + 1) % 2]], writes=[CC[i]])
                P.op(P.dve, lambda e, i=i: e.tensor_scalar(out=c8[:], in0=cc[i][:], scalar1=8.0, scalar2=None, op0=ALU.mult), reads=[CC[i]], writes=[C8])
                for j in range(3):
                    P.op(P.dve, lambda e, j=j: e.tensor_copy(out=hi[j][:], in_=c8[:]), reads=[C8], writes=[HI[j]])
                    if j < 2:
                        P.op(P.dve, lambda e, j=j: e.tensor_copy(out=h32[:], in_=hi[j][:]), reads=[HI[j]], writes=[H32])
                        P.op(P.dve, lambda e: e.tensor_tensor(out=c8[:], in0=c8[:], in1=h32[:], op=ALU.subtract), reads=[C8, H32], writes=[C8])
                    P.dma(P.pool, lambda e, j=j, ts=ts: e.dma_start(out=self.cqT[:, 64 + j, ts], in_=hi[j][:]), HI[j],
                          reads=[HI[j]])
            if 'tm' not in self.skip:
                for q in range(4):
                    vb = (t * 4 + q) % 2
                    qs = slice(q * 128, (q + 1) * 128)
                    tok = slice(t * T + q * 128, t * T + (q + 1) * 128)
                    fl = []
                    for (c0, n, o0) in ((C_AV, 64, 0), (C_BV, 256, 64)):
                        fl += [(lambda e, k=k, c0=c0, n=n, o0=o0, qs=qs: e.matmul(psv[0][:, o0:o0 + n], lhsT=u[:, k, qs], rhs=Win[:, k, c0:c0 + n],
                                                                           start=(k == 0), stop=(k == 7))) for k in range(8)]
                    fl += [(lambda e, k=k, qs=qs: e.matmul(psv[0][:, 320:328], lhsT=u[:, k, qs], rhs=Win[:, k, C_IW:C_IW + 8],
                                                    start=(k == 0), stop=(k == 7))) for k in range(8)]
                    P.op(P.pe, fl, reads=WI + [U], writes=[PSV[0]])
                    P.op(P.pe, [(lambda e, k=k, qs=qs: e.matmul(psv[1][:, 0:512], lhsT=u[:, k, qs], rhs=Win[:, k, C_CV:C_CV + 512],
                                                         start=(k == 0), stop=(k == 7))) for k in range(8)], reads=WI + [U], writes=[PSV[1]])
                    P.op(P.act, lambda e, vb=vb: e.activation(out=vst[vb][:, 0:320], in_=psv[0][:, 0:320], func=AF.Copy), reads=[PSV[0]], writes=[VST[vb]])
                    P.op(P.dve, lambda e, vb=vb: e.tensor_copy(out=iwst[vb][:], in_=psv[0][:, 320:328]), reads=[PSV[0]], writes=[IWST[vb]])
                    P.op(P.dve, lambda e, vb=vb: e.tensor_copy(out=vst[vb][:, 320:832], in_=psv[1][:, 0:512]), reads=[PSV[1]], writes=[VST[vb]])
                    P.dma(P.pool, [lambda e, vb=vb, tok=tok: e.dma_start(out=self.av[tok, :], in_=vst[vb][:, 0:64]),
                                   lambda e, vb=vb, tok=tok: e.dma_start(out=self.bv[tok, :], in_=vst[vb][:, 64:320]),
                                   lambda e, vb=vb, tok=tok: e.dma_start(out=self.cv[tok, :], in_=vst[vb][:, 320:832])],
                          VST[vb], reads=[VST[vb]])
                    P.dma(P.pool, lambda e, vb=vb, tok=tok: e.dma_start(out=self.iw[tok, :], in_=iwst[vb][:]), IWST[vb],
                          reads=[IWST[vb]])
                    if 'fg' not in self.skip and 'nc' not in self.skip:
                        P.op(P.pe, lambda e, i=i, qs=qs: e.transpose(ps3[:, 0:8], cc[i][0:8, qs], self.ident32[0:8, 0:8]), reads=[CC[i], self.CF], writes=[PS3])
                        P.op(P.act, lambda e, vb=vb: e.activation(out=ncst[vb][:], in_=ps3[:, 0:8], func=AF.Copy, scale=-1.0), reads=[PS3], writes=[NCST[vb]])
                        P.dma(P.pool, lambda e, vb=vb, tok=tok: e.dma_start(out=self.negc[tok, :], in_=ncst[vb][:]), NCST[vb],
                              reads=[NCST[vb]])
        P.end_phase()

    def attn_phase(self, kind):
        P, S = self.P, self.S
        NT = S // 128
        VC = min(16, NT)
        NQ = S // 512
        fox = kind == "fox"
        NH = 8 if fox else 4
        KA = 67 if fox else 96
        mix0 = 512 if fox else 256
        qT = self.cqT if fox else self.bqT
        kT = self.ckT if fox else self.bkT
        vsc = self.cv if fox else self.bv
        P.begin_phase()
        self.consts()
        Qa = [P.sbuf(f"Qa{i}", [KA, S], BF16) for i in range(2)]; QA = [Buf(f"Qa{i}") for i in range(2)]
        QS = [Buf(f"Qs{i}") for i in range(2)]
        Ka = [P.sbuf(f"Ka{i}", [KA, S], BF16) for i in range(2)]; KAB = [Buf(f"Ka{i}") for i in range(2)]
        Vt = [P.sbuf(f"Vt{i}", [128, NT, 65], BF16) for i in range(2)]; VB = [Buf(f"Vt{i}") for i in range(2)]
        VO = [Buf(f"Vo{i}") for i in range(2)]
        pT = [P.sbuf(f"pT{i}", [128, 512], BF16) for i in range(4)]; PT = [Buf(f"pT{i}") for i in range(4)]
        st = [P.psum(f"st{i}", [128, 512], F32) for i in range(4)]; ST = [Buf(f"st{i}") for i in range(4)]
        oT = [P.psum(f"oT{i}", [128, 512], F32) for i in range(2)]; OT = [Buf(f"oT{i}") for i in range(2)]
        bc = P.psum("bc", [128, 512], F32); BC = Buf("bc")
        rrow = P.sbuf("rrow", [65, 512], F32); RRW = Buf("rrow")
        bcs = P.sbuf("bcs", [64, 512], F32); BCS = Buf("bcs")
        osb = [P.sbuf(f"osb{i}", [64, 512], BF16) for i in range(2)]; OSB = [Buf(f"osb{i}") for i in range(2)]
        ones32 = P.sbuf("ones32", [65, 64], F32); ON32 = Buf("ones32")
        P.op(P.pool, lambda e: e.memset(ones32[:], 1.0), writes=[ON32])
        for i in range(2):
            P.op(P.pool, lambda e, i=i: e.memset(Vt[i][:, :, 64:65], 1.0), writes=[VO[i]])
        if fox:
            NC = P.sbuf("NC", [128, NT, 8], F32); NCB = Buf("NC")
            P.dma(P.sp, lambda e: e.dma_start(out=NC[:], in_=self.negc.rearrange("(t p) h -> p t h", p=128)), NCB,
                  writes=[NCB])
            gt = [P.sbuf(f"gt{i}", [64, 512], BF16) for i in range(2)]; GT = [Buf(f"gt{i}") for i in range(2)]
            KO = [Buf(f"ko{i}") for i in range(2)]
            for i in range(2):
                P.op(P.pool, lambda e, i=i: e.memset(Ka[i][64:67, :], 1.0), writes=[KO[i]])
        else:
            KO = [Buf(f"ko{i}") for i in range(2)]
            for i in range(2):
                P.dma(P.sp, lambda e, i=i: e.dma_start(out=Ka[i][64:96, :], in_=self.c_E[:, :]), KO[i], writes=[KO[i]])
            km = P.sbuf("km", [64, 32], F32); KM = Buf("km")
            kmb = P.sbuf("kmb", [64, 32], BF16); KMB = Buf("kmb")
            G = P.sbuf("G", [128, NT * 32], F32); GB = Buf("G")
            m8 = P.sbuf("m8", [128, NT * 8], F32); M8 = Buf("m8")
            selbig = P.sbuf("selbig", [128, NT, 128], BF16); SELP = Buf("selbig")
            gm = P.sbuf("gm", [128, 3, NT * 32], F32); GM = Buf("gm")
            gps = P.psum("gps", [128, 512], F32); GPS = Buf("gps")
            P.op(P.pool, lambda e: e.memset(selbig[:], 0.0), writes=[SELP])
            P.op(P.pool, lambda e: e.memset(km[:], 0.0), writes=[KM])
            P.dma(P.sp, lambda e: e.dma_start(out=gm[:], in_=self.c_gm.rearrange("p (a n) -> p a n", a=3)), GM, writes=[GM])
        for h in range(NH):
            b = h % 2
            P.dma(P.sp, lambda e, b=b, h=h: e.dma_start(out=Qa[b][0:(67 if fox else 64), :], in_=qT[h, :, :]), QA[b],
                  writes=[QA[b]])
            P.dma(P.sp, lambda e, b=b, h=h: e.dma_start(out=Ka[b][0:64, :], in_=kT[h, :, :]), KAB[b],
                  writes=[KAB[b]])
            vv = vsc.rearrange("(t p) c -> p t c", p=128)
            P.dma(P.sp, [lambda e, b=b, h=h, t0=t0: e.dma_start(out=Vt[b][:, t0:t0 + VC, 0:64], in_=vv[:, t0:t0 + VC, h * 64:(h + 1) * 64])
                         for t0 in range(0, NT, VC)], VB[b], writes=[VB[b]])
            if not fox:
                P.op(P.dve, lambda e, b=b: e.tensor_reduce(out=km[:, 0:S // 256], in_=Ka[b][0:64, :].rearrange("p (n k) -> p n k", k=256),
                                                          axis=AX.X, op=ALU.add), reads=[KAB[b]], writes=[KM])
                P.op(P.dve, lambda e: e.tensor_copy(out=kmb[:], in_=km[:]), reads=[KM], writes=[KMB])
                for g in range(0, NT, 16):
                    ng = min(16, NT - g)
                    P.op(P.pe, [(lambda e, q=q, g=g, b=b: e.matmul(gps[:, q * 32:(q + 1) * 32], lhsT=Qa[b][0:64, (g + q) * 128:(g + q + 1) * 128],
                                                                  rhs=kmb[:, 0:32], start=True, stop=True)) for q in range(ng)],
                         reads=[QA[b], KMB], writes=[GPS])
                    P.op(P.dve, lambda e, g=g, ng=ng: e.tensor_tensor(out=G[:, g * 32:(g + ng) * 32], in0=gps[:, 0:ng * 32],
                                                                      in1=gm[:, 0, g * 32:(g + ng) * 32], op=ALU.add),
                         reads=[GPS, GM], writes=[GB])
                P.op(P.dve, [(lambda e, qt=qt: e.max(out=m8[:, qt * 8:(qt + 1) * 8], in_=G[:, qt * 32:(qt + 1) * 32])) for qt in range(NT)],
                     reads=[GB], writes=[M8])
                P.op(P.dve, [(lambda e, qt=qt: e.tensor_scalar(out=selbig[:, qt, 64:96], in0=G[:, qt * 32:(qt + 1) * 32],
                                                               scalar1=m8[:, qt * 8 + 2:qt * 8 + 3], scalar2=NEG, op0=ALU.is_lt, op1=ALU.mult))
                             for qt in range(NT)], reads=[GB, M8], writes=[SELP])
                selv = selbig[:, :, 64:96]
                P.op(P.dve, lambda e: e.tensor_tensor(out=selv, in0=selv, in1=gm[:, 1, :].rearrange("p (t n) -> p t n", n=32), op=ALU.mult),
                     reads=[SELP, GM], writes=[SELP])
                P.op(P.dve, lambda e: e.tensor_tensor(out=selv, in0=selv, in1=gm[:, 2, :].rearrange("p (t n) -> p t n", n=32), op=ALU.add),
                     reads=[SELP, GM], writes=[SELP])
                for g4 in range(0, NT, 4):
                    n4 = min(4, NT - g4)
                    P.op(P.pe, [(lambda e, q=q, g4=g4: e.matmul(gps[:, q * 128:(q + 1) * 128], lhsT=selbig[:, g4 + q, :], rhs=self.ident,
                                                                start=True, stop=True)) for q in range(n4)],
                         reads=[SELP, self.CB], writes=[GPS])
                    P.op(P.act, lambda e, b=b, g4=g4, n4=n4: e.activation(out=Qa[b][64:96, g4 * 128:(g4 + n4) * 128], in_=gps[64:96, 0:n4 * 128],
                                                                          func=AF.Copy), reads=[GPS], writes=[QS[b]])
            pairs = [(j, kt) for j in range(NQ) for kt in range(4 * j + 4)]

            def emit_qk(pi, j, kt, b=b):
                a = kt - 4 * j
                col0 = max(a, 0) * 128
                sb_ = pi % 4
                ks = slice(kt * 128, (kt + 1) * 128)
                fl = [lambda e: e.matmul(st[sb_][:, col0:512], lhsT=Ka[b][0:KA, ks], rhs=Qa[b][0:KA, j * 512 + col0:(j + 1) * 512],
                                         start=True, stop=(a < 0))]
                if a >= 0:
                    fl.append(lambda e: e.matmul(st[sb_][:, col0:col0 + 128], lhsT=self.ident, rhs=self.trineg, start=False, stop=True))
                P.op(P.pe, fl, reads=[KAB[b], KO[b], QA[b], QS[b], self.CB], writes=[ST[sb_]])

            def emit_rest(pi, j, kt, b=b, h=h):
                a = kt - 4 * j
                col0 = max(a, 0) * 128
                sb_ = pi % 4
                pb_ = pi % 4
                nk = 4 * j + 4
                ob_ = (h * NQ + j) % 2
                if kt == 0 and fox:
                    P.dma(P.sp, lambda e: e.dma_start(out=gt[ob_][:], in_=self.cgT[h * 64:(h + 1) * 64, j * 512:(j + 1) * 512]),
                          GT[ob_], writes=[GT[ob_]])
                if fox:
                    P.op(P.act, lambda e: e.activation(out=pT[pb_][:, col0:512], in_=st[sb_][:, col0:512], func=AF.Exp,
                                                       bias=NC[:, kt, h:h + 1], scale=0.125),
                         reads=[ST[sb_], NCB], writes=[PT[pb_]])
                else:
                    P.op(P.act, lambda e: e.activation(out=pT[pb_][:, col0:512], in_=st[sb_][:, col0:512], func=AF.Exp,
                                                       bias=self.zero, scale=0.125),
                         reads=[ST[sb_], self.SM], writes=[PT[pb_]])
                P.op(P.pe, lambda e: e.matmul(oT[ob_][0:65, col0:512], lhsT=Vt[b][:, kt, 0:65], rhs=pT[pb_][:, col0:512],
                                              start=(kt == 0), stop=(kt == nk - 1)),
                     reads=[VB[b], VO[b], PT[pb_]], writes=[OT[ob_]])
                if kt == nk - 1:
                    P.op(P.dve, lambda e: e.reciprocal(out=rrow[64:65, :], in_=oT[ob_][64:65, :]), reads=[OT[ob_]], writes=[RRW])
                    P.op(P.pe, lambda e: e.matmul(bc[0:64, :], lhsT=ones32[64:65, :], rhs=rrow[64:65, :], start=True, stop=True),
                         reads=[ON32, RRW], writes=[BC])
                    P.op(P.act, lambda e: e.activation(out=bcs[:], in_=bc[0:64, :], func=AF.Copy), reads=[BC], writes=[BCS])
                    if fox:
                        P.op(P.dve, lambda e: e.tensor_tensor(out=bcs[:], in0=bcs[:], in1=gt[ob_][:], op=ALU.mult),
                             reads=[BCS, GT[ob_]], writes=[BCS])
                    P.op(P.dve, lambda e: e.tensor_tensor(out=osb[ob_][:], in0=oT[ob_][0:64, :], in1=bcs[:], op=ALU.mult),
                         reads=[OT[ob_], BCS], writes=[OSB[ob_]])
                    P.dma(P.pool, lambda e: e.dma_start(out=self.mixT[mix0 + h * 64:mix0 + (h + 1) * 64, j * 512:(j + 1) * 512],
                                                        in_=osb[ob_][:]), OSB[ob_], reads=[OSB[ob_]])

            emit_qk(0, *pairs[0])
            emit_qk(1, *pairs[1])
            for pi, (j, kt) in enumerate(pairs):
                if pi + 2 < len(pairs):
                    emit_qk(pi + 2, *pairs[pi + 2])
                emit_rest(pi, j, kt)
        P.end_phase()

    def dsa_phase(self):
        P, S = self.P, self.S
        NT = S // 128
        VC = min(16, NT)
        P.begin_phase()
        self.consts()
        ik = P.sbuf("ik", [64, S], BF16); IK = Buf("ik")
        ak = P.sbuf("ak", [64, S], BF16); AK = Buf("ak")
        Vt = P.sbuf("Vt", [128, NT, 65], BF16); VB = Buf("Vt")
        P.dma(P.sp, lambda e: e.dma_start(out=ik[:], in_=self.ikT[:, :]), IK, writes=[IK])
        P.dma(P.sp, lambda e: e.dma_start(out=ak[:], in_=self.akT[:, :]), AK, writes=[AK])
        vv = self.av.rearrange("(t p) c -> p t c", p=128)
        P.op(P.pool, lambda e: e.memset(Vt[:, :, 64:65], 1.0), writes=[VB])
        P.dma(P.sp, [lambda e, t0=t0: e.dma_start(out=Vt[:, t0:t0 + VC, 0:64], in_=vv[:, t0:t0 + VC, :]) for t0 in range(0, NT, VC)],
              VB, writes=[VB])
        iq = [P.sbuf(f"iq{i}", [64, 8, 128], BF16) for i in range(2)]; IQ = [Buf(f"iq{i}") for i in range(2)]
        aq = [P.sbuf(f"aq{i}", [64, 4, 128], BF16) for i in range(2)]; AQ = [Buf(f"aq{i}") for i in range(2)]
        wq = [P.sbuf(f"wq{i}", [128, 8], F32) for i in range(2)]; WQ = [Buf(f"wq{i}") for i in range(2)]
        Dg = P.sbuf("Dg", [128, 8, 128], BF16); DG = Buf("Dg")
        R = [P.sbuf(f"R{i}", [128, 512], BF16) for i in range(3)]; RB = [Buf(f"R{i}") for i in range(3)]
        score = [P.sbuf(f"score{i}", [128, S], F32) for i in range(2)]; SC = [Buf(f"score{i}") for i in range(2)]
        mneg = [P.sbuf(f"mneg{i}", [128, S], BF16) for i in range(2)]; MN = [Buf(f"mneg{i}") for i in range(2)]
        junk = P.sbuf("junk", [128, S], BF16); JK = Buf("junk")
        junkA = P.sbuf("junkA", [128, (S * 5) // 16 + 64], BF16); JKA = Buf("junkA")
        sa = [P.sbuf(f"sa{i}", [128, 1], F32) for i in range(2)]; SA = [Buf(f"sa{i}") for i in range(2)]
        sv = P.sbuf("sv", [128, 8], F32); SV = Buf("sv")
        Hs = P.sbuf("Hs", [128, NSTEP + 1], F32); HS = Buf("Hs")
        tt = [P.sbuf(f"tt{i}", [128, 1], F32) for i in range(2)]; TT = [Buf(f"tt{i}") for i in range(2)]
        cn = [P.sbuf(f"cn{i}", [128, 1], F32) for i in range(2)]; CN = [Buf(f"cn{i}") for i in range(2)]
        uu = P.sbuf("uu", [128, 1], F32); UU = Buf("uu")
        pT = [P.sbuf(f"pT{i}", [128, 512], BF16) for i in range(3)]; PT = [Buf(f"pT{i}") for i in range(3)]
        psL = [P.psum(f"psL{i}", [128, 512], F32) for i in range(2)]; PSL = [Buf(f"psL{i}") for i in range(2)]
        psA = P.psum("psA", [128, 512], F32); PSA = Buf("psA")
        st = [P.psum(f"st{i}", [128, 512], F32) for i in range(2)]; ST = [Buf(f"st{i}") for i in range(2)]
        oT = P.psum("oT", [128, 512], F32); OT = Buf("oT")
        bc = P.psum("bc", [128, 512], F32); BC = Buf("bc")
        rrow = P.sbuf("rrow", [65, 512], F32); RRW = Buf("rrow")
        bcs = P.sbuf("bcs", [64, 512], F32); BCS = Buf("bcs")
        osb = [P.sbuf(f"osb{i}", [64, 512], BF16) for i in range(2)]; OSB = [Buf(f"osb{i}") for i in range(2)]
        ones32 = P.sbuf("ones32", [65, 64], F32); ON32 = Buf("ones32")
        P.op(P.pool, lambda e: e.memset(ones32[:], 1.0), writes=[ON32])
        iqv = self.iqT.rearrange("h d s -> d h s")
        aqv = self.aqT.rearrange("h d s -> d h s")
        mixv = self.mixT[0:256, :].rearrange("(h d) s -> d h s", d=64)
        cnt = {"r": 0, "p": 0}

        def stage_A(i):
            b = i % 2
            n = (i + 1) * 128
            qs = slice(i * 128, n)
            sc_, SCb = score[b], SC[b]
            P.dma(P.sp, lambda e: e.dma_start(out=iq[b][:], in_=iqv[:, :, qs]), IQ[b], writes=[IQ[b]])
            P.dma(P.sp, lambda e: e.dma_start(out=wq[b][:], in_=self.iw[qs, :]), WQ[b], writes=[WQ[b]])
            P.op(P.pool, [(lambda e, h=h: e.tensor_scalar(out=Dg[:, h, :], in0=self.ident, scalar1=wq[b][:, h:h + 1], scalar2=None, op0=ALU.mult))
                          for h in range(8)], reads=[WQ[b], self.CB], writes=[DG])
            nch = (n + 511) // 512
            for c in range(nch):
                ncol = min(512, n - c * 512)
                cs_ = slice(c * 512, c * 512 + ncol)
                for h in range(8):
                    lb = h % 2
                    rb = cnt["r"] % 3
                    cnt["r"] += 1
                    P.op(P.pe, lambda e, h=h, lb=lb, cs_=cs_, ncol=ncol: e.matmul(psL[lb][:, 0:ncol], lhsT=iq[b][:, h, :], rhs=ik[:, cs_],
                                                                                start=True, stop=True),
                         reads=[IQ[b], IK], writes=[PSL[lb]])
                    P.op(P.act, lambda e, lb=lb, rb=rb, ncol=ncol: e.activation(out=R[rb][:, 0:ncol], in_=psL[lb][:, 0:ncol], func=AF.Relu),
                         reads=[PSL[lb]], writes=[RB[rb]])
                    P.op(P.pe, lambda e, h=h, rb=rb, ncol=ncol: e.matmul(psA[:, 0:ncol], lhsT=Dg[:, h, :], rhs=R[rb][:, 0:ncol],
                                                                        start=(h == 0), stop=(h == 7)),
                         reads=[DG, RB[rb]], writes=[PSA])
                P.op(P.act, lambda e, cs_=cs_, ncol=ncol: e.activation(out=sc_[:, cs_], in_=psA[:, 0:ncol], func=AF.Copy),
                     reads=[PSA], writes=[SCb])

        def stage_B(i):
            b = i % 2
            n = (i + 1) * 128
            sc_, SCb = score[b], SC[b]
            P.op(P.dve, lambda e: e.tensor_reduce(out=sv[:, 0:1], in_=sc_[:, 0:n], axis=AX.X, op=ALU.max), reads=[SCb], writes=[SV])
            P.op(P.dve, lambda e: e.tensor_reduce(out=sv[:, 1:2], in_=sc_[:, 0:n], axis=AX.X, op=ALU.min), reads=[SCb], writes=[SV])
            P.op(P.dve, lambda e: e.scalar_tensor_tensor(out=sv[:, 2:3], in0=sv[:, 1:2], scalar=-1.0, in1=sv[:, 0:1], op0=ALU.mult, op1=ALU.max),
                 reads=[SV], writes=[SV])
            P.op(P.dve, lambda e: e.tensor_scalar(out=sv[:, 3:4], in0=sv[:, 2:3], scalar1=-1.001, scalar2=-1e-20, op0=ALU.mult, op1=ALU.add),
                 reads=[SV], writes=[SV])
            P.op(P.dve, lambda e: e.tensor_scalar(out=sv[:, 4:5], in0=sv[:, 3:4], scalar1=-2.0, scalar2=None, op0=ALU.mult), reads=[SV], writes=[SV])
            P.op(P.dve, lambda e: e.tensor_scalar(out=Hs[:], in0=self.p2, scalar1=sv[:, 4:5], scalar2=None, op0=ALU.mult),
                 reads=[SV, self.CF], writes=[HS])
            P.op(P.pool, lambda e: e.tensor_tensor(out=sc_[:, n - 128:n], in0=sc_[:, n - 128:n], in1=self.causfill, op=ALU.add),
                 reads=[SCb, self.CF], writes=[SCb])
            P.op(P.dve, lambda e: e.tensor_tensor(out=tt[0][:], in0=sv[:, 3:4], in1=Hs[:, 0:1], op=ALU.add), reads=[SV, HS], writes=[TT[0]])
            na = 0
            nd = n - na
            for k in range(NSTEP):
                a_, b_ = k % 2, (k + 1) % 2
                P.op(P.dve, lambda e, a_=a_: e.tensor_scalar(out=junk[:, 0:nd], in0=sc_[:, 0:nd], scalar1=tt[a_][:, 0:1], scalar2=0.0,
                                                            op0=ALU.is_ge, op1=ALU.add, accum_out=cn[a_][:, 0:1]),
                     reads=[SCb, TT[a_]], writes=[JK, CN[a_]])
                if na:
                    P.op(P.act, lambda e, a_=a_: e.activation(out=junkA[:, 0:na], in_=sc_[:, nd:n], func=AF.Sign, bias=tt[a_][:, 0:1], scale=-1.0,
                                                             accum_out=sa[a_][:, 0:1]),
                         reads=[SCb, TT[a_]], writes=[JKA, SA[a_]])
                    P.op(P.dve, lambda e, a_=a_: e.scalar_tensor_tensor(out=cn[a_][:], in0=cn[a_][:], scalar=2.0, in1=sa[a_][:],
                                                                       op0=ALU.mult, op1=ALU.subtract),
                         reads=[CN[a_], SA[a_]], writes=[CN[a_]])
                    cth = 2 * TOPK - 1.0 - na
                else:
                    cth = TOPK - 0.5
                P.op(P.dve, lambda e, k=k, a_=a_, cth=cth: e.scalar_tensor_tensor(out=uu[:], in0=cn[a_][:], scalar=cth, in1=Hs[:, k:k + 1], op0=ALU.is_ge, op1=ALU.mult),
                     reads=[CN[a_], HS], writes=[UU])
                P.op(P.dve, lambda e, k=k, a_=a_, b_=b_: e.scalar_tensor_tensor(out=tt[b_][:], in0=uu[:], scalar=Hs[:, k + 1:k + 2], in1=tt[a_][:],
                                                                               op0=ALU.subtract, op1=ALU.add),
                     reads=[UU, HS, TT[a_]], writes=[TT[b_]])
            fin = NSTEP % 2
            P.op(P.dve, lambda e: e.tensor_tensor(out=sv[:, 5:6], in0=tt[fin][:], in1=Hs[:, NSTEP:NSTEP + 1], op=ALU.subtract),
                 reads=[TT[fin], HS], writes=[SV])
            P.op(P.dve, lambda e: e.tensor_scalar(out=mneg[b][:, 0:n], in0=sc_[:, 0:n], scalar1=sv[:, 5:6], scalar2=NEG, op0=ALU.is_lt, op1=ALU.mult),
                 reads=[SCb, SV], writes=[MN[b]])
            if self.dbg and i == NT - 1:
                P.dma(P.sp, lambda e: e.dma_start(out=self.d_score[:, :], in_=sc_[:]), SCb, reads=[SCb])
                P.dma(P.sp, lambda e: e.dma_start(out=self.d_sv[:, :], in_=sv[:]), SV, reads=[SV])
                P.dma(P.sp, lambda e: e.dma_start(out=self.d_mneg[:, :], in_=mneg[b][:]), MN[b], reads=[MN[b]])

        def stage_C(i):
            b = i % 2
            n = (i + 1) * 128
            qs = slice(i * 128, n)
            P.dma(P.sp, lambda e: e.dma_start(out=aq[b][:], in_=aqv[:, :, qs]), AQ[b], writes=[AQ[b]])

            def qk(kt):
                sb_ = kt % 2
                ks = slice(kt * 128, (kt + 1) * 128)
                P.op(P.pe, [lambda e: e.matmul(st[sb_][:, 0:512], lhsT=mneg[b][:, ks], rhs=self.ident4, start=True, stop=False),
                            lambda e: e.matmul(st[sb_][:, 0:512], lhsT=ak[:, ks], rhs=aq[b][:].rearrange("d h q -> d (h q)"),
                                               start=False, stop=True)],
                     reads=[MN[b], AK, AQ[b], self.CB], writes=[ST[sb_]])
            qk(0)
            for kt in range(i + 1):
                sb_ = kt % 2
                pb_ = cnt["p"] % 3
                cnt["p"] += 1
                if kt + 1 <= i:
                    qk(kt + 1)
                P.op(P.act, lambda e, sb_=sb_, pb_=pb_: e.activation(out=pT[pb_][:], in_=st[sb_][:], func=AF.Exp, bias=self.zero, scale=0.125),
                     reads=[ST[sb_], self.SM], writes=[PT[pb_]])
                P.op(P.pe, lambda e, kt=kt, pb_=pb_: e.matmul(oT[0:65, 0:512], lhsT=Vt[:, kt, 0:65], rhs=pT[pb_][:, 0:512],
                                                             start=(kt == 0), stop=(kt == i)),
                     reads=[VB, PT[pb_]], writes=[OT])
            P.op(P.dve, lambda e: e.reciprocal(out=rrow[64:65, :], in_=oT[64:65, :]), reads=[OT], writes=[RRW])
            P.op(P.pe, lambda e: e.matmul(bc[0:64, :], lhsT=ones32[64:65, :], rhs=rrow[64:65, :], start=True, stop=True),
                 reads=[ON32, RRW], writes=[BC])
            P.op(P.act, lambda e: e.activation(out=bcs[:], in_=bc[0:64, :], func=AF.Copy), reads=[BC], writes=[BCS])
            P.op(P.dve, lambda e: e.tensor_tensor(out=osb[b][:], in0=oT[0:64, :], in1=bcs[:], op=ALU.mult), reads=[OT, BCS], writes=[OSB[b]])
            P.dma(P.pool, lambda e: e.dma_start(out=mixv[:, :, qs], in_=osb[b][:].rearrange("d (h q) -> d h q", h=4)), OSB[b],
                  reads=[OSB[b]])

        stage_A(0)
        for i in range(NT):
            if i + 1 < NT:
                stage_A(i + 1)
            stage_B(i)
            if i >= 1:
                stage_C(i - 1)
        stage_C(NT - 1)
        P.end_phase()

    def build(self, phases=None):
        P = self.P
        src = self.xT
        ph = phases or ("f1", "proj", "dsa", "moba", "fox", "f2")
        for l in range(self.depth):
            if "f1" in ph:
                self.ffn_phase(l, 1, src, False)
            src = self.yT
            if "proj" in ph:
                self.proj_phase(l)
            if "dsa" in ph:
                self.dsa_phase()
            if "moba" in ph:
                self.attn_phase("moba")
            if "fox" in ph:
                self.attn_phase("fox")
            if "f2" in ph:
                self.ffn_phase(l, 2, self.yT, True)
        P.replay()
        print("instr counts:", {e.name: (e.nins, len(e.ops)) for e in P.engs}, "sems:", P.nsem, flush=True)
        return self.nc


def _rot_perm(width):
    idx = np.arange(width)
    return (idx // 64) * 64 + ((idx % 64) + 32) % 64


def host_consts(S):
    bf = ml_dtypes.bfloat16
    inv = (1.0 / (np.float32(10000.0) ** (np.arange(0, 64, 2, dtype=np.float32) / np.float32(64)))).astype(np.float32)
    ang = (np.arange(S, dtype=np.float32)[:, None] * inv[None, :]).astype(np.float32)
    cos, sin = np.cos(ang).astype(np.float32), np.sin(ang).astype(np.float32)
    p = np.arange(128)
    c_cos = np.ascontiguousarray(cos[:, p % 32].T)
    sgn = np.where((p % 64) < 32, -1.0, 1.0).astype(np.float32)
    c_sin = np.ascontiguousarray((sin[:, p % 32] * sgn[None, :]).T).astype(np.float32)
    ident = np.eye(128, dtype=np.float32)
    blk = np.zeros((128, 128), np.float32); blk[:64, :64] = 1 / 64; blk[64:, 64:] = 1 / 64
    onesD = np.full((128, 128), 1 / 1024, np.float32)
    srow = np.arange(128)[:, None]; qcol = np.arange(128)[None, :]
    trineg = np.where(srow > qcol, NEG, 0.0).astype(np.float32)
    ones = np.ones((128, 128), np.float32)
    ident4 = np.tile(ident, (1, 4))
    c_bf = np.concatenate([ident, blk, onesD, trineg, ones, ident4], axis=1).astype(bf)
    causfill = np.where(qcol > srow, -1e30, 0.0).astype(np.float32)
    p2 = np.tile((2.0 ** -(np.arange(NSTEP + 1) + 1.0)).astype(np.float32)[None, :], (128, 1))
    c_f32 = np.concatenate([ident, causfill, p2], axis=1).astype(np.float32)
    E = (np.arange(S)[None, :] // 256 == np.arange(32)[:, None]).astype(np.float32).astype(bf)
    NT = S // 128
    own = (np.arange(NT) // 2)[:, None]
    nn = np.arange(32)[None, :]
    gfill = np.where(nn < own, 0.0, -1e30)
    m1 = np.where(nn < own, 1.0, 0.0)
    m2 = np.where(nn > own, NEG, 0.0)
    gmrow = np.concatenate([gfill.reshape(-1), m1.reshape(-1), m2.reshape(-1)]).astype(np.float32)
    c_gm = np.ascontiguousarray(np.tile(gmrow[None, :], (128, 1)))
    return {"c_cos": c_cos, "c_sin": c_sin, "c_bf": c_bf, "c_f32": c_f32, "c_E": E, "c_gm": c_gm}


def host_layer(inp, l):
    g = lambda k: np.asarray(inp[k][l], np.float32)
    win = g("w_in")
    rot_cols = np.concatenate([s + _rot_perm(w) for s, w in ROT_SEGS])
    vecs = np.zeros((128, NVEC), np.float32)
    rp = _rot_perm(64)
    for col, key in ((V_AQ, "a_q_norm"), (V_AK, "a_k_norm"), (V_BQ, "b_q_norm"), (V_BK, "b_k_norm")):
        v = g(key)
        vecs[:, col] = np.tile(v, 2)
        vecs[:, col + 1] = np.tile(v[rp], 2)
    vecs[:, V_CQ] = np.tile(g("c_q_norm"), 2)
    vecs[:, V_CK] = np.tile(g("c_k_norm"), 2)
    vecs[:, V_N1:V_N1 + 8] = g("ffn1_norm").reshape(8, 128).T
    vecs[:, V_NM:V_NM + 8] = g("mix_norm").reshape(8, 128).T
    vecs[:, V_N2:V_N2 + 8] = g("ffn2_norm").reshape(8, 128).T
    vecs[0:8, V_FB] = g("forget_bias")
    return {f"f1g{l}": g("ffn1_w_gate"), f"f1u{l}": g("ffn1_w_up"), f"f1d{l}": g("ffn1_w_down"),
            f"win{l}": win, f"wrot{l}": np.ascontiguousarray(win[:, rot_cols]), f"wout{l}": g("w_out"),
            f"f2g{l}": g("ffn2_w_gate"), f"f2u{l}": g("ffn2_w_up"), f"f2d{l}": g("ffn2_w_down"), f"vecs{l}": vecs}


def kernel(**inputs):
    x = np.asarray(inputs["x"], np.float32)
    B, S, _ = x.shape
    depth = inputs["w_in"].shape[0]
    kb = K(S, depth)
    nc = kb.build()
    shared = host_consts(S)
    for l in range(depth):
        shared.update(host_layer(inputs, l))
    in_maps = []
    for b in range(B):
        m = dict(shared)
        m["xT"] = np.ascontiguousarray(x[b].T)
        in_maps.append(m)
    res = run_bass_kernel_spmd(nc, in_maps, core_ids=list(range(B)))
    out = np.stack([np.ascontiguousarray(r["yT"].T) for r in res.results], axis=0)
    return out.astype(np.float32)
```
